# Optimizing a Trainium2 kernel written in Bass

```python
import math
import jax, jax.numpy as jnp
from jax import lax
import numpy as np


D_MODEL = 1024
BATCH = 8
SEQ = 2048
DEPTH = 4

CHUNK = 64
N_META = 16
HEAD_DIM = 64
D_CONV = 256
CONV_WIDTH = 31
RWKV_HEADS = 6
D_RWKV = RWKV_HEADS * HEAD_DIM
DECAY_LORA = 64
AAA_LORA = 64
GATE_LORA = 128
SB_HEADS = 6
D_SB = SB_HEADS * HEAD_DIM
SB_BLOCK = 128
D_MIX = D_CONV + D_RWKV + D_SB
D_RWKV_IN = 3 * D_RWKV + DECAY_LORA + AAA_LORA + GATE_LORA
D_IN = 2 * D_CONV + D_RWKV_IN + 3 * D_SB
D_FF = 2816
RMS_EPS = 1e-6
LN_EPS = 1e-5
GN_EPS = 64e-5

kernel_name = 'hymba_conformer_rwkv7_stickbreak_macaron'


def rms_norm(x, g):
    xf = x.astype(jnp.float32)
    y = xf * lax.rsqrt(jnp.mean(xf * xf, axis=-1, keepdims=True) + RMS_EPS)
    return (y * g.astype(jnp.float32)).astype(x.dtype)


def swiglu_ffn(h, w13, w2):
    gate, up = jnp.split(h @ w13, 2, axis=-1)
    return (jax.nn.silu(gate) * up) @ w2


def conv_group(p_val, p_gate, conv_w, conv_b, ln_g, ln_b):
    c = p_val * jax.nn.sigmoid(p_gate)
    y = lax.conv_general_dilated(
        c, conv_w[:, None, :], window_strides=(1,), padding=[(CONV_WIDTH - 1, 0)],
        dimension_numbers=('NWC', 'WIO', 'NWC'), feature_group_count=D_CONV) + conv_b
    yf = y.astype(jnp.float32)
    mu = jnp.mean(yf, axis=-1, keepdims=True)
    var = jnp.mean(jnp.square(yf - mu), axis=-1, keepdims=True)
    yn = (yf - mu) * lax.rsqrt(var + LN_EPS) * ln_g.astype(jnp.float32) + ln_b.astype(jnp.float32)
    return jax.nn.silu(yn).astype(p_val.dtype)


def rwkv7_group(p, mu, w0, wB, a0, aB, gB, k_k, k_a, r_k, ln_g, ln_b):
    f32 = jnp.float32
    out_dtype = p.dtype
    p = p.astype(f32)
    B, L, _ = p.shape
    prev = jnp.pad(p[:, :-1], ((0, 0), (1, 0), (0, 0)))
    xs = p + mu.astype(f32) * (prev - p)
    r, k, v, wd, ad, gd = jnp.split(
        xs, [D_RWKV, 2 * D_RWKV, 3 * D_RWKV, 3 * D_RWKV + DECAY_LORA,
             3 * D_RWKV + DECAY_LORA + AAA_LORA], axis=-1)
    w_log = -jax.nn.softplus(-(w0.astype(f32) + jnp.tanh(wd) @ wB.astype(f32))) - 0.5
    decay = jnp.exp(-jnp.exp(w_log))
    a = jax.nn.sigmoid(a0.astype(f32) + ad @ aB.astype(f32))
    g = jax.nn.sigmoid(gd) @ gB.astype(f32)

    def heads(t):
        return t.reshape(B, L, RWKV_HEADS, HEAD_DIM)

    kk = heads(k * k_k.astype(f32))
    kk = kk / jnp.maximum(jnp.sqrt(jnp.sum(kk * kk, axis=-1, keepdims=True)), 1e-12)
    k = k * (1.0 + (a - 1.0) * k_a.astype(f32))
    rh, kh, vh, wh, ah = heads(r), heads(k), heads(v), heads(decay), heads(a)

    def step(S, inp):
        r_t, w_t, k_t, v_t, na_t, b_t = inp
        sa = jnp.einsum('bhvk,bhk->bhv', S, na_t)
        S = S * w_t[:, :, None, :] + sa[..., None] * b_t[:, :, None, :] + v_t[..., None] * k_t[:, :, None, :]
        return S, jnp.einsum('bhvk,bhk->bhv', S, r_t)

    seq_in = tuple(t.transpose(1, 0, 2, 3) for t in (rh, wh, kh, vh, -kk, kk * ah))
    S0 = jnp.zeros((B, RWKV_HEADS, HEAD_DIM, HEAD_DIM), f32)
    _, o = lax.scan(step, S0, seq_in)
    o = o.transpose(1, 0, 2, 3)
    m = jnp.mean(o, axis=-1, keepdims=True)
    var = jnp.mean(jnp.square(o - m), axis=-1, keepdims=True)
    o = ((o - m) * lax.rsqrt(var + GN_EPS)).reshape(B, L, D_RWKV)
    o = o * ln_g.astype(f32) + ln_b.astype(f32)
    bonus = jnp.sum(rh * kh * r_k.astype(f32), axis=-1, keepdims=True) * vh
    o = (o + bonus.reshape(B, L, D_RWKV)) * g
    return o.astype(out_dtype)


def stick_breaking_group(q, k, v, norm_g):
    f32 = jnp.float32
    B, L, _ = q.shape
    n_blk = -(-L // SB_BLOCK)
    Lp = n_blk * SB_BLOCK

    def heads(t):
        t = jnp.pad(t, ((0, 0), (0, Lp - L), (0, 0)))
        return t.reshape(B, Lp, SB_HEADS, HEAD_DIM).transpose(0, 2, 1, 3)

    qh, kh, vh = heads(q), heads(k), heads(v)
    scale = HEAD_DIM ** -0.5
    outs = []
    for i in range(n_blk):
        q0 = i * SB_BLOCK
        kend = q0 + SB_BLOCK
        qb = qh[:, :, q0:kend].astype(f32)
        kb = kh[:, :, :kend].astype(f32)
        z = jnp.einsum('bhqd,bhkd->bhqk', qb, kb) * scale
        t_idx = q0 + jnp.arange(SB_BLOCK)
        s_idx = jnp.arange(kend)
        causal = s_idx[None, :] < t_idx[:, None]
        log_rest = jnp.where(causal, jax.nn.log_sigmoid(-z), 0.0)
        after = lax.cumsum(log_rest, axis=3, reverse=True) - log_rest
        log_a = jnp.where(causal, jax.nn.log_sigmoid(z) + after, -jnp.inf)
        outs.append(jnp.einsum('bhqk,bhkd->bhqd', jnp.exp(log_a), vh[:, :, :kend].astype(f32)))
    o = jnp.concatenate(outs, axis=2)[:, :, :L].transpose(0, 2, 1, 3)
    o = o * lax.rsqrt(jnp.mean(o * o, axis=-1, keepdims=True) + RMS_EPS)
    o = o * norm_g.astype(f32).reshape(SB_HEADS, HEAD_DIM)
    return o.reshape(B, L, D_SB).astype(q.dtype)


def setup_inputs(seed: int = 0) -> dict:
    key = jax.random.key(seed)
    ks = iter(jax.random.split(key, 40))
    f32 = jnp.float32

    def nrm(shape, scale):
        return scale * jax.random.normal(next(ks), shape, f32)

    def gain(shape):
        return 1.0 + 0.02 * jax.random.normal(next(ks), shape, f32)

    def unif(shape, lo, hi):
        return jax.random.uniform(next(ks), shape, f32, lo, hi)

    return {
        'x': nrm((BATCH, SEQ, D_MODEL), 1.0),
        'meta': nrm((N_META, D_MODEL), 1.0),
        'ffn1_norm': gain((DEPTH, D_MODEL)),
        'ffn1_w13': nrm((DEPTH, D_MODEL, 2 * D_FF), D_MODEL ** -0.5),
        'ffn1_w2': nrm((DEPTH, D_FF, D_MODEL), D_FF ** -0.5),
        'mix_norm': gain((DEPTH, D_MODEL)),
        'w_in': nrm((DEPTH, D_MODEL, D_IN), D_MODEL ** -0.5),
        'conv_w': nrm((DEPTH, CONV_WIDTH, D_CONV), CONV_WIDTH ** -0.5),
        'conv_b': nrm((DEPTH, D_CONV), 0.02),
        'conv_ln_g': gain((DEPTH, D_CONV)),
        'conv_ln_b': nrm((DEPTH, D_CONV), 0.02),
        'rwkv_mu': unif((DEPTH, D_RWKV_IN), 0.0, 1.0),
        'rwkv_w0': unif((DEPTH, D_RWKV), -5.0, 1.0),
        'rwkv_wB': nrm((DEPTH, DECAY_LORA, D_RWKV), 0.1),
        'rwkv_a0': nrm((DEPTH, D_RWKV), 0.5),
        'rwkv_aB': nrm((DEPTH, AAA_LORA, D_RWKV), AAA_LORA ** -0.5),
        'rwkv_gB': nrm((DEPTH, GATE_LORA, D_RWKV), GATE_LORA ** -0.5),
        'rwkv_kk': 0.85 + nrm((DEPTH, D_RWKV), 0.05),
        'rwkv_ka': 1.0 + nrm((DEPTH, D_RWKV), 0.05),
        'rwkv_rk': nrm((DEPTH, RWKV_HEADS, HEAD_DIM), 0.1),
        'rwkv_ln_g': gain((DEPTH, D_RWKV)),
        'rwkv_ln_b': nrm((DEPTH, D_RWKV), 0.02),
        'sb_norm': gain((DEPTH, D_SB)),
        'w_out': nrm((DEPTH, D_MIX, D_MODEL), D_MIX ** -0.5),
        'ffn2_norm': gain((DEPTH, D_MODEL)),
        'ffn2_w13': nrm((DEPTH, D_MODEL, 2 * D_FF), D_MODEL ** -0.5),
        'ffn2_w2': nrm((DEPTH, D_FF, D_MODEL), D_FF ** -0.5),
        'final_norm': gain((D_MODEL,)),
    }


def reference(x, meta, ffn1_norm, ffn1_w13, ffn1_w2, mix_norm, w_in, conv_w, conv_b,
              conv_ln_g, conv_ln_b, rwkv_mu, rwkv_w0, rwkv_wB, rwkv_a0, rwkv_aB, rwkv_gB,
              rwkv_kk, rwkv_ka, rwkv_rk, rwkv_ln_g, rwkv_ln_b, sb_norm, w_out,
              ffn2_norm, ffn2_w13, ffn2_w2, final_norm):
    B = x.shape[0]
    meta_b = jnp.broadcast_to(meta.astype(x.dtype)[None], (B, N_META, D_MODEL))
    h = jnp.concatenate([meta_b, x], axis=1)
    split_pts = [D_CONV, 2 * D_CONV, 2 * D_CONV + D_RWKV_IN,
                 2 * D_CONV + D_RWKV_IN + D_SB, 2 * D_CONV + D_RWKV_IN + 2 * D_SB]
    for l in range(DEPTH):
        h = h + 0.5 * swiglu_ffn(rms_norm(h, ffn1_norm[l]), ffn1_w13[l], ffn1_w2[l])
        p = rms_norm(h, mix_norm[l]) @ w_in[l]
        pc_val, pc_gate, p_rwkv, q_sb, k_sb, v_sb = jnp.split(p, split_pts, axis=-1)
        y_conv = conv_group(pc_val, pc_gate, conv_w[l], conv_b[l], conv_ln_g[l], conv_ln_b[l])
        y_rwkv = rwkv7_group(p_rwkv, rwkv_mu[l], rwkv_w0[l], rwkv_wB[l], rwkv_a0[l], rwkv_aB[l],
                             rwkv_gB[l], rwkv_kk[l], rwkv_ka[l], rwkv_rk[l], rwkv_ln_g[l], rwkv_ln_b[l])
        y_sb = stick_breaking_group(q_sb, k_sb, v_sb, sb_norm[l])
        y = jnp.concatenate([y_conv, y_rwkv, y_sb], axis=-1)
        h = h + y @ w_out[l]
        h = h + 0.5 * swiglu_ffn(rms_norm(h, ffn2_norm[l]), ffn2_w13[l], ffn2_w2[l])
    h = rms_norm(h, final_norm)
    return h[:, N_META:]
```

```python
import numpy as np
import concourse.bass as bass
import concourse.mybir as mybir
from concourse.bass_utils import run_bass_kernel_spmd

F32 = mybir.dt.float32
BF16 = mybir.dt.bfloat16
AF = mybir.ActivationFunctionType
ALU = mybir.AluOpType
NDS = 24

NL_FULL = 4
D = 1024
NMETA = 16
SEQ = 2048
LP = 2176
TILES = [(0, 512), (512, 512), (1024, 512), (1536, 512), (2048, 128)]
GROUPS = [[0, 1], [2, 3, 4]]
DFF = 2816
NFC = 22
NPV = 128
P_F1N, P_MXN, P_F2N, P_CW, P_CB, P_CLG, P_CLB = 0, 8, 16, 24, 86, 88, 90
P_MU, P_W0, P_A0, P_KK, P_KA, P_RK, P_LG, P_LB, P_SBN = 92, 103, 106, 109, 112, 115, 118, 121, 124
C_ID, C_ONES, C_BONES, C_NTRI, C_NONES, C_SBM, C_CMU, C_CML, C_I64 = 0, 128, 256, 384, 512, 640, 2688, 2816, 2880
NCST = 2944
ARENA_WORDS = 23296
RMS_EPS = 1e-6
LN_EPS = 1e-5
GN_EPS = 64e-5
DECAY_C = 0.6065306597126334


class T:
    __slots__ = ("w", "r", "excl")

    def __init__(self, excl=False):
        self.w = None
        self.r = {}
        self.excl = excl


class Sched:
    ENG = ["pe", "act", "dve", "pool", "sp"]

    def __init__(self, nc):
        self.nc = nc
        self.sem = {e: nc.alloc_semaphore(name=f"s_{e}") for e in self.ENG}
        self.cnt = {e: 0 for e in self.ENG}
        self.prog = {e: [] for e in self.ENG}
        self.seen = {e: {} for e in self.ENG}
        self.dsems = [nc.alloc_semaphore(name=f"dq{i}") for i in range(NDS)]
        self.dcnt = [0] * NDS
        self.dnext = {"pool": 0, "sp": 0}

    def _deps(self, e, reads, writes, need):
        def req(ev, raw):
            if ev is None:
                return
            key, sem, val = ev
            if key == e and e == "pe":
                return
            if key not in need or need[key][1] < val:
                need[key] = (sem, val)

        for t in reads:
            req(t.w, True)
            if t.excl:
                for k_, ev in t.r.items():
                    if k_ != e:
                        req(ev, False)
        for t in writes:
            req(t.w, False)
            for ev in t.r.values():
                req(ev, False)
        waits = []
        for key, (sem, val) in need.items():
            if self.seen[e].get(key, 0) < val:
                self.seen[e][key] = val
                waits.append((sem, val))
        return waits

    def op(self, e, fn, reads=(), writes=()):
        waits = self._deps(e, reads, writes, {})
        self.cnt[e] += 1
        ev = (e, self.sem[e], self.cnt[e])
        self.prog[e].append((waits, fn, (self.sem[e], 1)))
        for t in reads:
            t.r[e] = ev
        for t in writes:
            t.w = ev
            t.r = {}
        return ev

    def dma(self, q, fn, reads=(), writes=()):
        half = NDS // 2
        k_ = self.dnext[q]
        self.dnext[q] = (k_ + 1) % half
        i = k_ + (half if q == "pool" else 0)
        sem = self.dsems[i]
        key = ("d", i)
        need = {}
        if self.dcnt[i] > 0:
            need[key] = (sem, self.dcnt[i] * 16)
        self.dcnt[i] += 1
        val = self.dcnt[i] * 16
        waits = self._deps(q, reads, writes, need)
        self.prog[q].append((waits, fn, (sem, 16)))
        ev = (key, sem, val)
        for t in reads:
            t.r[key] = ev
        for t in writes:
            t.w = ev
            t.r = {}
        return ev

    def wait_events(self, e, evs):
        waits = []
        for key, sem, val in evs:
            if self.seen[e].get(key, 0) < val:
                self.seen[e][key] = val
                waits.append((sem, val))
        self.prog[e].append((waits, None, None))

    def fence(self, e):
        if self.cnt[e] > self.seen[e].get(e, 0):
            self.seen[e][e] = self.cnt[e]
            self.prog[e].append(([(self.sem[e], self.cnt[e])], None, None))

    def barrier(self):
        evs = [(e, self.sem[e], self.cnt[e]) for e in self.ENG if self.cnt[e] > 0]
        evs += [(("d", i), self.dsems[i], self.dcnt[i] * 16) for i in range(NDS) if self.dcnt[i] > 0]
        for e in self.ENG:
            self.wait_events(e, [ev for ev in evs if ev[0] != e])

    def emit(self):
        with self.nc.Block() as block:
            decs = {"pe": block.tensor, "act": block.scalar, "dve": block.vector,
                    "pool": block.gpsimd, "sp": block.sync}
            for name in self.ENG:
                prog = self.prog[name]

                def body(eng, prog=prog):
                    for waits, fn, inc in prog:
                        for sem, val in waits:
                            eng.wait_ge(sem, val)
                        if fn is not None:
                            ins = fn(eng)
                            if inc is not None:
                                ins.then_inc(inc[0], inc[1])

                decs[name](body)


class Ring:
    def __init__(self, aps):
        self.aps = aps
        self.ts = [T() for _ in aps]
        self.i = 0

    def next(self):
        i = self.i
        self.i = (i + 1) % len(self.aps)
        return self.aps[i], self.ts[i]


class Builder:
    def __init__(self, nl, do_mixer=True, dbg=None, branches=("conv", "rwkv", "sb")):
        self.nl = nl
        self.branches = branches
        self.dbg = dbg
        self.do_mixer = do_mixer
        nc = bass.Bass("TRN2", target_bir_lowering=False)
        self.nc = nc
        self.S = Sched(nc)
        d = {}

        def din(name, shape):
            d[name] = nc.dram_tensor(name, shape, F32, kind="ExternalInput").ap()

        din("xpad", [LP, D])
        din("pvec", [128, nl * NPV + 8])
        din("cst", [128, NCST])
        for w in ("ffn1", "ffn2"):
            din(f"{w}_w13", [nl, D, 2 * DFF])
            din(f"{w}_w2", [nl, DFF, D])
        din("w_in", [nl, D, 3072])
        din("w_out", [nl, D, D])
        din("rwkv_wB", [nl, 64, 384])
        din("rwkv_aB", [nl, 64, 384])
        din("rwkv_gB", [nl, 128, 384])
        self.d = d
        self.out = nc.dram_tensor("out", [SEQ, D], F32, kind="ExternalOutput").ap()
        if dbg:
            self.dbg_out = nc.dram_tensor("dbg", [128, 8, LP], F32, kind="ExternalOutput").ap()
        self.H = nc.alloc_sbuf_tensor("H", [128, 8, LP], F32)
        self.XN = nc.alloc_sbuf_tensor("XN", [128, 8, LP], BF16)
        self.CF = nc.alloc_sbuf_tensor("CF", [128, 384], F32)
        self.CB = nc.alloc_sbuf_tensor("CB", [128, NCST], BF16)
        self.PV = nc.alloc_sbuf_tensor("PV", [128, nl * NPV + 8], F32)
        self.PD = nc.alloc_sbuf_tensor("PD", [128, nl * 8 + 8], F32)
        self.ARENA = nc.alloc_sbuf_tensor("ARENA", [128, ARENA_WORDS], F32)
        self.pb = [nc.alloc_psum_tensor(f"pb{i}", [128, 512], F32) for i in range(8)]
        self.pbT = [T(excl=True) for _ in range(8)]
        self.HT = [T() for _ in TILES]
        self.XNT = [T() for _ in TILES]
        self.cT = T()
        self.aoff = 0

    def aalloc(self, shape, dt):
        n = int(np.prod(shape))
        words = (n * (4 if dt == F32 else 2) + 3) // 4
        ap = self.ARENA[:, self.aoff:self.aoff + words]
        self.aoff += words
        assert self.aoff <= ARENA_WORDS, (self.aoff, ARENA_WORDS)
        if dt == BF16:
            ap = ap.bitcast(BF16)
        if len(shape) == 2:
            ap = ap.rearrange("p (a b) -> p a b", b=shape[1])
        elif len(shape) == 3:
            ap = ap.rearrange("p (a b c) -> p a b c", b=shape[1], c=shape[2])
        return ap

    def ring(self, n, shape, dt):
        return Ring([self.aalloc(shape, dt) for _ in range(n)])

    def mm(self, out, lhsT, rhs, start=True, stop=True, r=(), w=(), sgc=False):
        return self.S.op("pe", lambda e: e.matmul(out, lhsT, rhs, start=start, stop=stop, skip_group_check=sgc),
                         reads=r, writes=w)

    def tr(self, out, in_, ident, r=(), w=()):
        return self.S.op("pe", lambda e: e.transpose(out, in_, ident), reads=r, writes=w)

    def act(self, out, in_, func, bias=None, scale=None, r=(), w=()):
        kw = {}
        if bias is not None:
            kw["bias"] = bias
        if scale is not None:
            kw["scale"] = scale
        return self.S.op("act", lambda e: e.activation(out=out, in_=in_, func=func, **kw), reads=r, writes=w)

    def tt(self, out, in0, in1, op, r=(), w=(), eng="dve"):
        return self.S.op(eng, lambda e: e.tensor_tensor(out=out, in0=in0, in1=in1, op=op), reads=r, writes=w)

    def ts(self, out, in0, s1, s2, op0, op1=None, r=(), w=(), eng="dve"):
        if op1 is None:
            return self.S.op(eng, lambda e: e.tensor_scalar(out=out, in0=in0, scalar1=s1, scalar2=None, op0=op0),
                             reads=r, writes=w)
        return self.S.op(eng, lambda e: e.tensor_scalar(out=out, in0=in0, scalar1=s1, scalar2=s2, op0=op0, op1=op1),
                         reads=r, writes=w)

    def stt(self, out, in0, scalar, in1, op0, op1, r=(), w=()):
        return self.S.op("dve", lambda e: e.scalar_tensor_tensor(out=out, in0=in0, scalar=scalar, in1=in1,
                                                                 op0=op0, op1=op1), reads=r, writes=w)

    def cp(self, out, in_, r=(), w=(), eng="dve"):
        return self.S.op(eng, lambda e: e.tensor_copy(out=out, in_=in_), reads=r, writes=w)

    def ld(self, out, in_, w=(), q="sp", r=()):
        return self.S.dma(q, lambda e: e.dma_start(out=out, in_=in_), reads=r, writes=w)

    def pvc(self, l, col, n=1):
        c = l * NPV + col
        return self.PV[:, c:c + n]

    def rstd_from(self, out, ps_in, scale, eps_col, r, w):
        self.act(out, ps_in, AF.Ln, bias=self.PD[:, eps_col:eps_col + 1], scale=scale, r=list(r) + [self.cT], w=w)
        self.act(out, out, AF.Exp, scale=-0.5, r=w, w=w)

    def prologue(self):
        nl = self.nl
        self.ld(self.PV[:], self.d["pvec"], w=[self.cT])
        self.ld(self.CF[:], self.d["cst"][:, 0:384], w=[self.cT])
        self.ld(self.CB[:], self.d["cst"], w=[self.cT], q="pool")
        e0 = nl * 8
        self.S.op("dve", lambda e: e.memset(self.PD[:, e0 + 0:e0 + 1], RMS_EPS), writes=[self.cT])
        self.S.op("dve", lambda e: e.memset(self.PD[:, e0 + 1:e0 + 2], LN_EPS), writes=[self.cT])
        self.S.op("dve", lambda e: e.memset(self.PD[:, e0 + 2:e0 + 3], GN_EPS), writes=[self.cT])
        self.S.op("dve", lambda e: e.memset(self.PD[:, e0 + 3:e0 + 4], 1.0), writes=[self.cT])
        self.S.op("dve", lambda e: e.memset(self.PD[:, e0 + 4:e0 + 5], 0.0), writes=[self.cT])
        for l in range(nl):
            self.ts(self.PD[:, l * 8:l * 8 + 3], self.pvc(l, P_KA, 3), -1.0, 1.0, ALU.mult, ALU.add,
                    r=[self.cT], w=[self.cT])
        self.EPS_RMS, self.EPS_LN, self.EPS_GN, self.ONE, self.ZERO = e0, e0 + 1, e0 + 2, e0 + 3, e0 + 4
        self.aoff = 0
        xr = self.ring(3, [D], F32)
        ident = self.CF[:, 0:128]
        k = 0
        for blk in range(LP // 128):
            xa, xt = xr.next()
            self.ld(xa, self.d["xpad"][blk * 128:(blk + 1) * 128, :], w=[xt])
            ti = next(i for i, (t0, tw) in enumerate(TILES) if t0 <= blk * 128 < t0 + tw)
            for half in range(2):
                pb, pt = self.pb[k % 8], self.pbT[k % 8]
                k += 1
                for c in range(4):
                    cc = half * 4 + c
                    self.tr(pb[:, c * 128:(c + 1) * 128], xa[:, cc * 128:(cc + 1) * 128], ident,
                            r=[xt, self.cT], w=[pt])
                self.cp(self.H[:, half * 4:half * 4 + 4, blk * 128:(blk + 1) * 128],
                        pb[:, :].rearrange("p (a b) -> p a b", b=128), r=[pt], w=[self.HT[ti]],
                        eng=("dve" if half == 0 else "act") if False else "dve")
        self.S.barrier()

    def rmsnorm_tile(self, ti, gcol_l, sqring, rstd, rstdT, xn_dst, xn_T, tok_off=0):
        t0, tw = TILES[ti]
        sp, spT = self.pb[6 + (ti % 2)], self.pbT[6 + (ti % 2)]
        ones = self.CB[:, C_ONES:C_ONES + 128]
        for c in range(8):
            sq, sqT = sqring.next()
            self.act(sq[:, 0:tw], self.H[:, c, t0:t0 + tw], AF.Square, r=[self.HT[ti]], w=[sqT])
            self.mm(sp[:, 0:tw], ones, sq[:, 0:tw], start=(c == 0), stop=(c == 7), r=[sqT, self.cT], w=[spT])
        self.rstd_from(rstd[:, 0:tw], sp[:, 0:tw], 1.0 / D, self.EPS_RMS, r=[spT], w=[rstdT])
        for c in range(8):
            self.stt(xn_dst[:, c, tok_off:tok_off + tw], self.H[:, c, t0:t0 + tw], gcol_l[:, c:c + 1],
                     rstd[:, 0:tw], ALU.mult, ALU.mult, r=[self.HT[ti], rstdT, self.cT], w=[xn_T])

    def setup_ffn_arena(self):
        self.aoff = 0
        A = {}
        A["g"] = self.aalloc([NFC, 1152], BF16)
        A["gT"] = [T() for _ in range(3)]
        A["w13"] = self.ring(2, [8, 2, 256], BF16)
        A["w2"] = self.ring(2, [NFC, 128], BF16)
        A["sq"] = self.ring(4, [512], BF16)
        A["rstd"] = self.aalloc([512], F32)
        A["rstdT"] = T()
        A["sg"] = self.ring(2, [512], F32)
        self.FA = A

    def ffn(self, l, which):
        A = self.FA
        w13 = self.d[f"ffn{which}_w13"]
        w2 = self.d[f"ffn{which}_w2"]
        gcol = self.pvc(l, P_F1N if which == 1 else P_F2N, 8)
        for ti in range(len(TILES)):
            self.rmsnorm_tile(ti, gcol, A["sq"], A["rstd"], A["rstdT"], self.XN, self.XNT[ti], tok_off=TILES[ti][0])
        for grp in GROUPS:
            g0 = TILES[grp[0]][0]
            for fp in range(NFC // 2):
                wa, wt = A["w13"].next()
                f0 = fp * 256
                self.ld(wa[:, :, 0, :], w13[l, :, f0:f0 + 256].rearrange("(c p) f -> p c f", p=128), w=[wt], q="pool")
                self.ld(wa[:, :, 1, :], w13[l, :, DFF + f0:DFF + f0 + 256].rearrange("(c p) f -> p c f", p=128),
                        w=[wt], q="pool")
                for fi in range(2):
                    fc = fp * 2 + fi
                    for j, ti in enumerate(grp):
                        t0, tw = TILES[ti]
                        pg, pgT = self.pb[j % 2], self.pbT[j % 2]
                        pu, puT = self.pb[2 + j % 2], self.pbT[2 + j % 2]
                        for c in range(8):
                            self.mm(pg[:, 0:tw], wa[:, c, 0, fi * 128:(fi + 1) * 128], self.XN[:, c, t0:t0 + tw],
                                    start=(c == 0), stop=(c == 7), r=[wt, self.XNT[ti]], w=[pgT])
                        for c in range(8):
                            self.mm(pu[:, 0:tw], wa[:, c, 1, fi * 128:(fi + 1) * 128], self.XN[:, c, t0:t0 + tw],
                                    start=(c == 0), stop=(c == 7), r=[wt, self.XNT[ti]], w=[puT])
                        sg, sgT = A["sg"].next()
                        self.act(sg[:, 0:tw], pg[:, 0:tw], AF.Silu, r=[pgT], w=[sgT])
                        self.tt(A["g"][:, fc, t0 - g0:t0 - g0 + tw], pu[:, 0:tw], sg[:, 0:tw], ALU.mult,
                                r=[puT, sgT], w=[A["gT"][j]])
            for dc in range(8):
                wa, wt = A["w2"].next()
                self.ld(wa, w2[l, :, dc * 128:(dc + 1) * 128].rearrange("(c p) f -> p c f", p=128), w=[wt], q="pool")
                for j, ti in enumerate(grp):
                    t0, tw = TILES[ti]
                    po, poT = self.pb[4 + j % 2], self.pbT[4 + j % 2]
                    for fc in range(NFC):
                        self.mm(po[:, 0:tw], wa[:, fc, :], A["g"][:, fc, t0 - g0:t0 - g0 + tw],
                                start=(fc == 0), stop=(fc == NFC - 1), r=[wt, A["gT"][j]], w=[poT])
                    self.stt(self.H[:, dc, t0:t0 + tw], po[:, 0:tw], 0.5, self.H[:, dc, t0:t0 + tw],
                             ALU.mult, ALU.add, r=[poT, self.HT[ti]], w=[self.HT[ti]])

    def epilogue(self):
        nl = self.nl
        self.S.barrier()
        self.aoff = 0
        sqr = self.ring(4, [512], BF16)
        rstd = self.aalloc([512], F32)
        rstdT = T()
        yn = self.ring(2, [8, 512], F32)
        ob = self.ring(3, [D], F32)
        gcol = self.PV[:, nl * NPV:nl * NPV + 8]
        ones = self.CB[:, C_ONES:C_ONES + 128]
        ident = self.CF[:, 0:128]
        evs = []
        k = 0
        for ti, (t0, tw) in enumerate(TILES):
            ya, yt = yn.next()
            sp, spT = self.pb[6 + (ti % 2)], self.pbT[6 + (ti % 2)]
            for c in range(8):
                sq, sqT = sqr.next()
                self.act(sq[:, 0:tw], self.H[:, c, t0:t0 + tw], AF.Square, r=[self.HT[ti]], w=[sqT])
                self.mm(sp[:, 0:tw], ones, sq[:, 0:tw], start=(c == 0), stop=(c == 7), r=[sqT, self.cT], w=[spT])
            self.rstd_from(rstd[:, 0:tw], sp[:, 0:tw], 1.0 / D, self.EPS_RMS, r=[spT], w=[rstdT])
            for c in range(8):
                self.stt(ya[:, c, 0:tw], self.H[:, c, t0:t0 + tw], gcol[:, c:c + 1], rstd[:, 0:tw],
                         ALU.mult, ALU.mult, r=[self.HT[ti], rstdT, self.cT], w=[yt])
            for b in range(tw // 128):
                tok = t0 + b * 128
                oa, ot = ob.next()
                for half in range(2):
                    pb, pt = self.pb[k % 6], self.pbT[k % 6]
                    k += 1
                    for c in range(4):
                        cc = half * 4 + c
                        self.tr(pb[:, c * 128:(c + 1) * 128], ya[:, cc, b * 128:(b + 1) * 128], ident,
                                r=[yt, self.cT], w=[pt])
                    self.cp(oa[:, half * 512:(half + 1) * 512], pb[:, :], r=[pt], w=[ot])
                lo = max(tok, NMETA)
                hi = min(tok + 128, NMETA + SEQ)
                if hi > lo:
                    evs.append(self.ld(self.out[lo - NMETA:hi - NMETA, :], oa[lo - tok:hi - tok, :], r=[ot]))
        self.S.wait_events("sp", evs)

    def dump(self, src_fn):
        evs = []
        self.S.barrier()
        for c in range(8):
            evs.append(self.ld(self.dbg_out[:, c, :], src_fn(c)))
        self.S.wait_events("sp", evs)
        self.S.barrier()

    def build(self):
        self.prologue()
        self.setup_ffn_arena()
        for l in range(self.nl):
            self.ffn(l, 1)
            if self.do_mixer:
                self.S.barrier()
                self.mixer(l)
                self.S.barrier()
            self.ffn(l, 2)
        if self.dbg == "h":
            self.dump(lambda c: self.H[:, c, :])
        self.epilogue()
        self.S.emit()
        return self.nc


    def nb(self):
        i = self._nb % self._nbn
        self._nb = i + 1
        return self.pb[i], self.pbT[i]

    def wout(self, l, row0, nch, yT, yTT, wo, woT, banks=None):
        w_out = self.d["w_out"]
        kk = [0]

        def nbk():
            if banks is None:
                return self.nb()
            b_ = banks[kk[0] % len(banks)]
            kk[0] += 1
            return self.pb[b_], self.pbT[b_]

        self.ld(wo[:, 0:nch, :], w_out[l, row0:row0 + nch * 128, :].rearrange("(c p) f -> p c f", p=128),
                w=[woT], q="pool")
        for dc in range(8):
            for ti, (t0, tw) in enumerate(TILES):
                po, poT = nbk()
                for c in range(nch):
                    self.mm(po[:, 0:tw], wo[:, c, dc * 128:(dc + 1) * 128], yT[:, c, t0:t0 + tw],
                            start=(c == 0), stop=(c == nch - 1), r=[woT, yTT[c][ti]], w=[poT])
                self.tt(self.H[:, dc, t0:t0 + tw], po[:, 0:tw], self.H[:, dc, t0:t0 + tw], ALU.add,
                        r=[poT, self.HT[ti]], w=[self.HT[ti]])

    def mixer(self, l):
        self._nb = 0
        self._nbn = 7
        self.aoff = 0
        sq = self.ring(4, [512], BF16)
        rstd = self.aalloc([512], F32)
        rstdT = T()
        gcol = self.pvc(l, P_MXN, 8)
        for ti in range(len(TILES)):
            self.rmsnorm_tile(ti, gcol, sq, rstd, rstdT, self.XN, self.XNT[ti], tok_off=TILES[ti][0])
        br = self.branches
        if "conv" in br:
            self.S.barrier()
            self.conv_branch(l)
        if "rwkv" in br:
            self.S.barrier()
            self.rwkv_branch(l)
        if "sb" in br:
            self.S.barrier()
            self.sb_branch(l)

    def conv_branch(self, l):
        self.aoff = 0
        w_in = self.d["w_in"]
        yT = self.aalloc([2, LP], BF16)
        yTT = [[T() for _ in TILES] for _ in range(2)]
        wo = self.aalloc([2, D], BF16)
        woT = T()
        wc = self.aalloc([8, 512], BF16)
        wcT = T()
        self.ld(wc, w_in[l, :, 0:512].rearrange("(c p) f -> p c f", p=128), w=[wcT], q="pool")
        cbuf = self.aalloc([2, 30 + LP], BF16)
        cbT = [T() for _ in TILES]
        cpadT = T()
        diag = self.aalloc([62, 128], BF16)
        diagT = T()
        yc = self.ring(2, [2, 512], F32)
        ysq = self.aalloc([2, 512], F32)
        ysqT = T()
        sg = self.ring(2, [512], F32)
        mu = self.aalloc([512], F32); muT = T()
        var = self.aalloc([512], F32); varT = T()
        dr = self.ring(2, [512], F32)
        self.S.op("dve", lambda e: e.memset(cbuf[:, :, 0:30], 0.0), writes=[cpadT])
        identb = self.CB[:, C_ID:C_ID + 128]
        for k in range(62):
            self.ts(diag[:, k, :], identb, self.pvc(l, P_CW + k), None, ALU.mult, r=[self.cT], w=[diagT])
        for ti, (t0, tw) in enumerate(TILES):
            for cc in range(2):
                pv_, pvT = self.nb()
                pg_, pgT = self.nb()
                for c in range(8):
                    self.mm(pv_[:, 0:tw], wc[:, c, cc * 128:(cc + 1) * 128], self.XN[:, c, t0:t0 + tw],
                            start=(c == 0), stop=(c == 7), r=[wcT, self.XNT[ti]], w=[pvT])
                for c in range(8):
                    self.mm(pg_[:, 0:tw], wc[:, c, 256 + cc * 128:256 + (cc + 1) * 128], self.XN[:, c, t0:t0 + tw],
                            start=(c == 0), stop=(c == 7), r=[wcT, self.XNT[ti]], w=[pgT])
                sa, sT = sg.next()
                self.act(sa[:, 0:tw], pg_[:, 0:tw], AF.Sigmoid, r=[pgT], w=[sT])
                self.tt(cbuf[:, cc, 30 + t0:30 + t0 + tw], pv_[:, 0:tw], sa[:, 0:tw], ALU.mult,
                        r=[pvT, sT], w=[cbT[ti]])
        onesF = self.CF[:, C_ONES:C_ONES + 128]
        for ti, (t0, tw) in enumerate(TILES):
            ya, yt = yc.next()
            deps = [diagT, cbT[ti], cpadT] + ([cbT[ti - 1]] if ti > 0 else [])
            for cc in range(2):
                po, poT = self.nb()
                for j in range(31):
                    self.mm(po[:, 0:tw], diag[:, cc * 31 + j, :], cbuf[:, cc, t0 + j:t0 + j + tw],
                            start=(j == 0), stop=(j == 30), r=deps, w=[poT])
                self.act(ya[:, cc, 0:tw], po[:, 0:tw], AF.Identity, bias=self.pvc(l, P_CB + cc),
                         r=[poT, self.cT], w=[yt])
                self.act(ysq[:, cc, 0:tw], ya[:, cc, 0:tw], AF.Square, r=[yt], w=[ysqT])
            s1, s1T = self.nb()
            s2, s2T = self.nb()
            for cc in range(2):
                self.mm(s1[:, 0:tw], onesF, ya[:, cc, 0:tw], start=(cc == 0), stop=(cc == 1), r=[yt, self.cT], w=[s1T])
            for cc in range(2):
                self.mm(s2[:, 0:tw], onesF, ysq[:, cc, 0:tw], start=(cc == 0), stop=(cc == 1), r=[ysqT, self.cT], w=[s2T])
            self.act(mu[:, 0:tw], s1[:, 0:tw], AF.Copy, scale=1.0 / 256, r=[s1T], w=[muT])
            self.act(var[:, 0:tw], s1[:, 0:tw], AF.Square, scale=1.0 / 256, r=[s1T], w=[varT])
            self.stt(var[:, 0:tw], s2[:, 0:tw], 1.0 / 256, var[:, 0:tw], ALU.mult, ALU.subtract, r=[s2T, varT], w=[varT])
            self.rstd_from(var[:, 0:tw], var[:, 0:tw], 1.0, self.EPS_LN, r=[varT], w=[varT])
            for cc in range(2):
                da, dT = dr.next()
                self.tt(da[:, 0:tw], ya[:, cc, 0:tw], mu[:, 0:tw], ALU.subtract, r=[yt, muT], w=[dT])
                self.stt(da[:, 0:tw], da[:, 0:tw], self.pvc(l, P_CLG + cc), var[:, 0:tw], ALU.mult, ALU.mult,
                         r=[dT, varT, self.cT], w=[dT])
                self.act(yT[:, cc, t0:t0 + tw], da[:, 0:tw], AF.Silu, bias=self.pvc(l, P_CLB + cc),
                         r=[dT, self.cT], w=[yTT[cc][ti]])
        self.wout(l, 0, 2, yT, yTT, wo, woT)


    def proj_shift(self, l, wsl, wT, mucol, ti, carry, carryT, pbuf, tmpd, tmpdT, dst, dstT):
        t0, tw = TILES[ti]
        pb_, pt_ = self.nb()
        for c in range(8):
            self.mm(pb_[:, 0:tw], wsl(c), self.XN[:, c, t0:t0 + tw], start=(c == 0), stop=(c == 7),
                    r=[wT, self.XNT[ti]], w=[pt_])
        pa, paT = pbuf.next()
        if ti == 0:
            self.S.op("dve", lambda e: e.memset(pa[:, 0:1], 0.0), writes=[paT])
        else:
            self.act(pa[:, 0:1], carry, AF.Copy, r=[carryT], w=[paT])
        self.act(pa[:, 1:1 + tw], pb_[:, 0:tw], AF.Copy, r=[pt_], w=[paT])
        self.act(carry, pa[:, tw:tw + 1], AF.Copy, r=[paT], w=[carryT])
        self.tt(tmpd[:, 0:tw], pa[:, 0:tw], pa[:, 1:1 + tw], ALU.subtract, r=[paT], w=[tmpdT])
        self.stt(dst, tmpd[:, 0:tw], self.pvc(l, mucol), pa[:, 1:1 + tw], ALU.mult, ALU.add,
                 r=[tmpdT, paT, self.cT], w=[dstT])

    def rwkv_branch(self, l):
        self.aoff = 0
        self._nbn = 6
        w_in = self.d["w_in"]
        yT = self.aalloc([1, LP], BF16)
        wo = self.aalloc([1, D], BF16); woT = T()
        wl = self.aalloc([8, 256], BF16); wlT = T()
        LB = self.aalloc([384], BF16); LBT = T()
        GBw = self.aalloc([384], BF16); GBT = T()
        self.ld(wl, w_in[l, :, 1664:1920].rearrange("(c p) f -> p c f", p=128), w=[wlT], q="pool")
        self.ld(LB[0:64, :], self.d["rwkv_wB"][l], w=[LBT], q="pool")
        self.ld(LB[64:128, :], self.d["rwkv_aB"][l], w=[LBT], q="pool")
        self.ld(GBw, self.d["rwkv_gB"][l], w=[GBT], q="pool")
        lora1 = self.aalloc([LP], BF16)
        sgd = self.aalloc([LP], BF16)
        loraT = [T() for _ in TILES]
        pbuf = self.ring(1, [513], F32)
        carry = self.aalloc([12], F32)
        carryT = [T() for _ in range(12)]
        F = [self.aalloc([512], F32) for _ in range(9)]
        FT = [T() for _ in range(9)]
        tmpd, tmpdT = F[8], FT[8]
        csbuf = self.aalloc([576], F32); csT = T()
        xvb = self.aalloc([512], BF16); xvbT = T()
        kksq = self.aalloc([512], BF16); kksqT = T()
        BT = self.aalloc([512], BF16); BTT = T()
        KT = self.aalloc([512], BF16); KTT = T()
        NR = self.aalloc([8, 128], BF16); NRT = T()
        tm = self.aalloc([4, 8, 64], BF16); tmT = T()
        M1s = self.aalloc([8, 128], BF16); M1T = T()
        M2s = self.aalloc([8, 128], BF16); M2T = T()
        Pr = self.ring(2, [8, 64], BF16)
        Qr = self.ring(2, [8, 64], BF16)
        TTr = self.ring(2, [8, 64], BF16)
        Wtm = self.aalloc([8, 64], BF16); WtmT = T()
        Xs = self.aalloc([8, 64], BF16); XsT = T()
        Utb, UtbT = Xs, XsT
        GTb = [self.aalloc([8, 64], BF16) for _ in range(2)]; GTbT = [T(), T()]
        RWb = [self.aalloc([8, 64], BF16) for _ in range(2)]; RWbT = [T(), T()]
        Cpb = [self.aalloc([8, 64], F32) for _ in range(2)]; CpbT = [T(), T()]
        Ddb = [self.aalloc([8, 64], BF16) for _ in range(2)]; DdbT = [T(), T()]
        GCb = [self.aalloc([8], F32) for _ in range(2)]; GCbT = [T(), T()]
        gbufb = [self.aalloc([512], BF16) for _ in range(2)]; gbT = [T(), T()]
        bonb = [self.aalloc([512], BF16) for _ in range(2)]; bonT = [T(), T()]
        Sbr = self.ring(2, [64], BF16)
        FO = [self.aalloc([512], F32) for _ in range(3)]
        FOT = [T() for _ in range(3)]
        wr = self.aalloc([8, 3, 128], BF16); wrT = T()
        self.S.op("dve", lambda e: e.memset(csbuf[:, 0:64], 0.0), writes=[csT])
        for ti, (t0, tw) in enumerate(TILES):
            self.proj_shift(l, lambda c: wl[:, c, 0:128], wlT, P_MU + 9, ti, carry[:, 9:10], carryT[9],
                            pbuf, tmpd, tmpdT, F[0][:, 0:tw], FT[0])
            self.act(lora1[0:64, t0:t0 + tw], F[0][0:64, 0:tw], AF.Tanh, r=[FT[0]], w=[loraT[ti]])
            self.cp(lora1[64:128, t0:t0 + tw], F[0][64:128, 0:tw], r=[FT[0]], w=[loraT[ti]])
            self.proj_shift(l, lambda c: wl[:, c, 128:256], wlT, P_MU + 10, ti, carry[:, 10:11], carryT[10],
                            pbuf, tmpd, tmpdT, F[1][:, 0:tw], FT[1])
            self.act(sgd[:, t0:t0 + tw], F[1][:, 0:tw], AF.Sigmoid, r=[FT[1]], w=[loraT[ti]])
        self.arena_peak = max(getattr(self, 'arena_peak', 0), self.aoff)
        bones = self.CB[:, C_BONES:C_BONES + 128]
        bonesF = self.CF[:, C_BONES:C_BONES + 128]
        maskM = self.CB[:, C_CMU:C_CMU + 128]
        maskL = self.CB[:, C_CML:C_CML + 64]
        i64 = self.CB[:, C_I64:C_I64 + 64]
        R2 = [slice(0, 64), slice(64, 128)]
        Ob = [(self.pb[6], self.pbT[6]), (self.pb[7], self.pbT[7])]

        def both(emit):
            emit(0, R2[0])
            self.S.fence("pe")
            emit(1, R2[1])
            self.S.fence("pe")
        xr, xk, a_, sgw, t1, inv, ci, Ei, Ee = F
        xrT, xkT, aT, sgwT, t1T, invT, ciT, EiT, EeT = FT
        yTT = [[T() for _ in TILES]]
        v3 = lambda ap, n, b=64: ap.rearrange("p (a b) -> p a b", b=b)[:, 0:n, :]

        def prep(j, ti):
            t0, tw = TILES[ti]
            nck = tw // 64
            s = ti % 2
            v2 = lambda ap: ap[:, 0:tw].rearrange("p (a b) -> p a b", b=64)
            self.proj_shift(l, lambda c: wr[:, c, 0, :], wrT, P_MU + j, ti, carry[:, j:j + 1], carryT[j],
                            pbuf, tmpd, tmpdT, xr[:, 0:tw], xrT)
            yield
            self.proj_shift(l, lambda c: wr[:, c, 1, :], wrT, P_MU + 3 + j, ti, carry[:, 3 + j:4 + j], carryT[3 + j],
                            pbuf, tmpd, tmpdT, xk[:, 0:tw], xkT)
            yield
            self.proj_shift(l, lambda c: wr[:, c, 2, :], wrT, P_MU + 6 + j, ti, carry[:, 6 + j:7 + j], carryT[6 + j],
                            pbuf, tmpd, tmpdT, xvb[:, 0:tw], xvbT)
            yield
            pw, pwT = self.nb()
            self.mm(pw[:, 0:tw], LB[0:64, j * 128:(j + 1) * 128], lora1[0:64, t0:t0 + tw], r=[LBT, loraT[ti]], w=[pwT])
            self.act(sgw[:, 0:tw], pw[:, 0:tw], AF.Sigmoid, bias=self.pvc(l, P_W0 + j), r=[pwT, self.cT], w=[sgwT])
            pa2, pa2T = self.nb()
            self.mm(pa2[:, 0:tw], LB[64:128, j * 128:(j + 1) * 128], lora1[64:128, t0:t0 + tw],
                    r=[LBT, loraT[ti]], w=[pa2T])
            self.act(a_[:, 0:tw], pa2[:, 0:tw], AF.Sigmoid, bias=self.pvc(l, P_A0 + j), r=[pa2T, self.cT], w=[aT])
            pg2, pg2T = self.nb()
            self.mm(pg2[:, 0:tw], GBw[:, j * 128:(j + 1) * 128], sgd[:, t0:t0 + tw], r=[GBT, loraT[ti]], w=[pg2T])
            self.act(gbufb[s][:, 0:tw], pg2[:, 0:tw], AF.Copy, r=[pg2T], w=[gbT[s]])
            yield
            self.act(kksq[:, 0:tw], xk[:, 0:tw], AF.Square, scale=self.pvc(l, P_KK + j), r=[xkT, self.cT], w=[kksqT])
            pss, pssT = self.nb()
            self.mm(pss[:, 0:tw], bones, kksq[:, 0:tw], r=[kksqT, self.cT], w=[pssT])
            self.ts(inv[:, 0:tw], pss[:, 0:tw], 1e-24, None, ALU.max, r=[pssT], w=[invT])
            self.act(inv[:, 0:tw], inv[:, 0:tw], AF.Ln, r=[invT], w=[invT])
            self.act(inv[:, 0:tw], inv[:, 0:tw], AF.Exp, scale=-0.5, r=[invT], w=[invT])
            self.ts(t1[:, 0:tw], a_[:, 0:tw], self.pvc(l, P_KA + j), self.PD[:, l * 8 + j:l * 8 + j + 1],
                    ALU.mult, ALU.add, r=[aT, self.cT], w=[t1T])
            self.tt(t1[:, 0:tw], xk[:, 0:tw], t1[:, 0:tw], ALU.mult, r=[xkT, t1T], w=[t1T])
            self.stt(xk[:, 0:tw], xk[:, 0:tw], self.pvc(l, P_KK + j), inv[:, 0:tw], ALU.mult, ALU.mult,
                     r=[xkT, invT, self.cT], w=[xkT])
            yield
            self.S.op("dve", lambda e: e.tensor_tensor_scan(out=csbuf[:, 64:64 + tw], data0=sgw[:, 0:tw],
                                                            data1=sgw[:, 0:tw], initial=0.0,
                                                            op0=ALU.add, op1=ALU.bypass),
                      reads=[sgwT], writes=[csT])
            csv = csbuf[:, 64:64 + tw].rearrange("p (a b) -> p a b", b=64)
            base = csbuf[:, 0:tw].rearrange("p (a b) -> p a b", b=64)[:, :, 63:64].to_broadcast([128, nck, 64])
            self.tt(v2(ci), csv, base, ALU.subtract, r=[csT], w=[ciT])
            self.tt(Ee[:, 0:tw], ci[:, 0:tw], sgw[:, 0:tw], ALU.subtract, r=[ciT, sgwT], w=[EeT])
            self.act(Ee[:, 0:tw], Ee[:, 0:tw], AF.Exp, scale=-DECAY_C, r=[EeT], w=[EeT])
            self.act(Ei[:, 0:tw], ci[:, 0:tw], AF.Exp, scale=-DECAY_C, r=[ciT], w=[EiT])
            self.act(ci[:, 0:tw], ci[:, 0:tw], AF.Exp, scale=DECAY_C, r=[ciT], w=[ciT])
            yield
            En = ci
            self.stt(NR[:, 0:nck, 0:64], v2(xk), -1.0, v2(Ee), ALU.mult, ALU.mult, r=[xkT, EeT], w=[NRT])
            self.tt(NR[:, 0:nck, 64:128], v2(xr), v2(Ei), ALU.mult, r=[xrT, EiT], w=[NRT])
            self.tt(inv[:, 0:tw], xk[:, 0:tw], a_[:, 0:tw], ALU.mult, r=[xkT, aT], w=[invT])
            self.tt(BT[:, 0:tw], inv[:, 0:tw], En[:, 0:tw], ALU.mult, r=[invT, ciT], w=[BTT])
            self.tt(KT[:, 0:tw], t1[:, 0:tw], En[:, 0:tw], ALU.mult, r=[t1T, ciT], w=[KTT])
            self.act(GCb[s][:, 0:nck], v2(Ei)[:, :, 63], AF.Copy, r=[EiT], w=[GCbT[s]])
            self.stt(kksq[:, 0:tw], xr[:, 0:tw], self.pvc(l, P_RK + j), t1[:, 0:tw], ALU.mult, ALU.mult,
                     r=[xrT, t1T, self.cT], w=[kksqT])
            pbn, pbnT = self.nb()
            self.mm(pbn[:, 0:tw], bones, kksq[:, 0:tw], r=[kksqT, self.cT], w=[pbnT])
            self.tt(bonb[s][:, 0:tw], pbn[:, 0:tw], xvb[:, 0:tw], ALU.mult, r=[pbnT, xvbT], w=[bonT[s]])
            yield
            srcs = [(lambda c: NR[:, c, 0:64], NRT), (lambda c: BT[:, c * 64:(c + 1) * 64], BTT),
                    (lambda c: KT[:, c * 64:(c + 1) * 64], KTT), (lambda c: xvb[:, c * 64:(c + 1) * 64], xvbT)]
            for half in range(2):
                bk, bkT = self.nb()
                bkb = bk[:, :].bitcast(BF16)
                def em(e, rows, half=half, bkb=bkb, bkT=bkT):
                    for q in range(2):
                        fn, st = srcs[half * 2 + q]
                        for c in range(nck):
                            self.tr(bkb[rows, q * 512 + c * 64:q * 512 + (c + 1) * 64], fn(c)[rows, :],
                                    self.CB[rows, C_I64:C_I64 + 64], r=[st, self.cT], w=[bkT])
                both(em)
                self.cp(tm[:, half * 2:half * 2 + 2, 0:nck, :],
                        bkb.rearrange("p (q c k) -> p q c k", q=2, k=64)[:, :, 0:nck, :], r=[bkT], w=[tmT])
            yield
            nh = (nck + 3) // 4
            bM1 = [self.nb() for _ in range(nh)]
            bM2 = [self.nb() for _ in range(nh)]
            bA, bAT = self.nb()
            def em(e, rows):
                for c in range(nck):
                    cs_ = slice((c % 4) * 128, (c % 4) * 128 + 128)
                    self.mm(bM1[c // 4][0][rows, cs_], BT[rows, c * 64:(c + 1) * 64], NR[rows, c, :],
                            r=[BTT, NRT], w=[bM1[c // 4][1]])
                    self.mm(bM2[c // 4][0][rows, cs_], KT[rows, c * 64:(c + 1) * 64], NR[rows, c, :],
                            r=[KTT, NRT], w=[bM2[c // 4][1]])
                    self.mm(bA[rows, c * 64:(c + 1) * 64], NR[rows, c, 0:64], BT[rows, c * 64:(c + 1) * 64],
                            r=[BTT, NRT], w=[bAT])
            both(em)
            for hh in range(nh):
                n4 = min(4, nck - hh * 4)
                mb = maskM.unsqueeze(1).to_broadcast([128, n4, 128])
                self.tt(M1s[:, hh * 4:hh * 4 + n4, :], v3(bM1[hh][0][:, 0:n4 * 128], n4, 128),
                        mb, ALU.mult, r=[bM1[hh][1], self.cT], w=[M1T])
                self.tt(M2s[:, hh * 4:hh * 4 + n4, :], v3(bM2[hh][0][:, 0:n4 * 128], n4, 128),
                        mb, ALU.mult, r=[bM2[hh][1], self.cT], w=[M2T])
            P, PT_ = Pr.next()
            self.tt(P[:, 0:nck, :], v3(bA[:, 0:tw], nck), maskL.unsqueeze(1).to_broadcast([128, nck, 64]), ALU.mult,
                    r=[bAT, self.cT], w=[PT_])
            Q, QT_ = Qr.next()
            self.act(Q[:, 0:nck, :], M1s[:, 0:nck, 0:64], AF.Copy, r=[M1T], w=[QT_])
            TTa, TTT = TTr.next()
            self.tt(TTa[:, 0:nck, :], M1s[:, 0:nck, 0:64], i64.unsqueeze(1).to_broadcast([128, nck, 64]), ALU.add,
                    r=[M1T, self.cT], w=[TTT])
            yield
            for rnd in range(1, 7):
                doP = rnd <= 5
                doQ = rnd <= 4
                doT = rnd >= 2
                if doP:
                    bP, bPT = self.nb()
                if doQ:
                    bQ, bQT = self.nb()
                if doT:
                    bT, bTT = self.nb()
                def em(e, rows):
                    for c in range(nck):
                        cs_ = slice(c * 64, (c + 1) * 64)
                        if doP:
                            self.mm(bP[rows, cs_], Q[rows, c, :], P[rows, c, :], r=[QT_, PT_], w=[bPT])
                        if doQ:
                            self.mm(bQ[rows, cs_], P[rows, c, :], Q[rows, c, :], r=[QT_, PT_], w=[bQT])
                        if doT:
                            self.mm(bT[rows, cs_], P[rows, c, :], TTa[rows, c, :], r=[PT_, TTT], w=[bTT])
                both(em)
                if doT:
                    TT2, TT2T = TTr.next()
                    self.tt(TT2[:, 0:nck, :], v3(bT[:, 0:tw], nck), TTa[:, 0:nck, :], ALU.add, r=[bTT, TTT], w=[TT2T])
                    TTa, TTT = TT2, TT2T
                if doP:
                    P2, P2T = Pr.next()
                    self.cp(P2[:, 0:nck, :], v3(bP[:, 0:tw], nck), r=[bPT], w=[P2T])
                if doQ:
                    Q2, Q2T = Qr.next()
                    self.act(Q2[:, 0:nck, :], v3(bQ[:, 0:tw], nck), AF.Copy, r=[bQT], w=[Q2T])
                    Q, QT_ = Q2, Q2T
                if doP:
                    P, PT_ = P2, P2T
                yield
            bW, bWT = self.nb()
            bX, bXT = self.nb()
            def em(e, rows):
                for c in range(nck):
                    cs_ = slice(c * 64, (c + 1) * 64)
                    self.mm(bW[rows, cs_], TTa[rows, c, :], tm[rows, 0, c, :], r=[tmT, TTT], w=[bWT])
                    self.mm(bX[rows, cs_], M2s[rows, c, 0:64], tm[rows, 3, c, :], r=[tmT, M2T], w=[bXT])
            both(em)
            self.cp(Wtm[:, 0:nck, :], v3(bW[:, 0:tw], nck), r=[bWT], w=[WtmT])
            self.act(Xs[:, 0:nck, :], v3(bX[:, 0:tw], nck), AF.Copy, r=[bXT], w=[XsT])
            yield
            bU, bUT = self.nb()
            def em(e, rows):
                for c in range(nck):
                    self.mm(bU[rows, c * 64:(c + 1) * 64], TTa[rows, c, :], Xs[rows, c, :], r=[TTT, XsT], w=[bUT])
            both(em)
            self.act(Utb[:, 0:nck, :], v3(bU[:, 0:tw], nck), AF.Copy, r=[bUT], w=[UtbT])
            yield
            bG, bGT = self.nb()
            bR, bRT = self.nb()
            bC, bCT = self.nb()
            bD, bDT = self.nb()
            def em(e, rows):
                for c in range(nck):
                    cs_ = slice(c * 64, (c + 1) * 64)
                    self.mm(bG[rows, cs_], Wtm[rows, c, :], tm[rows, 1, c, :], r=[WtmT, tmT], w=[bGT])
                    self.mm(bR[rows, cs_], Wtm[rows, c, :], M1s[rows, c, 64:128], r=[WtmT, M1T], w=[bRT])
                    self.mm(bC[rows, cs_], tm[rows, 1, c, :], Utb[rows, c, :], start=True, stop=False, r=[tmT, UtbT], w=[bCT])
                    self.mm(bC[rows, cs_], tm[rows, 2, c, :], tm[rows, 3, c, :], start=False, stop=True, r=[tmT], w=[bCT])
                    self.mm(bD[rows, cs_], Utb[rows, c, :], M1s[rows, c, 64:128], start=True, stop=False, r=[UtbT, M1T], w=[bDT])
                    self.mm(bD[rows, cs_], tm[rows, 3, c, :], M2s[rows, c, 64:128], start=False, stop=True, r=[tmT, M2T], w=[bDT])
            both(em)
            self.tt(GTb[s][:, 0:nck, :], v3(bG[:, 0:tw], nck), i64.unsqueeze(1).to_broadcast([128, nck, 64]), ALU.add,
                    r=[bGT, self.cT], w=[GTbT[s]])
            self.tt(RWb[s][:, 0:nck, :], v3(bR[:, 0:tw], nck), NR[:, 0:nck, 64:128], ALU.add, r=[bRT, NRT], w=[RWbT[s]])
            self.tt(Cpb[s][:, 0:nck, :], v3(bC[:, 0:tw], nck), GCb[s][:, 0:nck].unsqueeze(2).to_broadcast([128, nck, 64]),
                    ALU.mult, r=[bCT, GCbT[s]], w=[CpbT[s]])
            self.act(Ddb[s][:, 0:nck, :], v3(bD[:, 0:tw], nck), AF.Copy, r=[bDT], w=[DdbT[s]])
            yield

        state = {}

        def seq(j, ti):
            t0, tw = TILES[ti]
            nck = tw // 64
            s = ti % 2
            Sb, SbT = state["S"]
            for c in range(nck):
                bSe = [self.nb(), self.nb()]
                Sn, SnT = Sbr.next()
                for e in range(2):
                    rows = R2[e]
                    self.mm(bSe[e][0][rows, 0:64], GTb[s][rows, c, :], Sb[rows, :], r=[GTbT[s], SbT], w=[bSe[e][1]])
                    self.mm(Ob[e][0][rows, c * 64:(c + 1) * 64], Sb[rows, :], RWb[s][rows, c, :], r=[SbT, RWbT[s]],
                            w=[Ob[e][1]])
                for e in range(2):
                    rows = R2[e]
                    self.stt(Sn[rows, :], bSe[e][0][rows, 0:64], GCb[s][rows, c:c + 1], Cpb[s][rows, c, :],
                             ALU.mult, ALU.add, r=[bSe[e][1], GCbT[s], CpbT[s]], w=[SnT])
                Sb, SbT = Sn, SnT
                yield
            state["S"] = (Sb, SbT)
            Osb, Osq, var = FO
            OsbT, OsqT, varT = FOT
            for e in range(2):
                rows = R2[e]
                self.tt(Osb[rows, 0:tw], Ob[e][0][rows, 0:tw], Ddb[s][rows, 0:nck, :].rearrange("p a b -> p (a b)"),
                        ALU.add, r=[Ob[e][1], DdbT[s]], w=[OsbT])
            self.act(Osq[:, 0:tw], Osb[:, 0:tw], AF.Square, r=[OsbT], w=[OsqT])
            s1, s1T = self.nb()
            s2, s2T = self.nb()
            self.mm(s1[:, 0:tw], bonesF, Osb[:, 0:tw], r=[OsbT, self.cT], w=[s1T])
            self.mm(s2[:, 0:tw], bonesF, Osq[:, 0:tw], r=[OsqT, self.cT], w=[s2T])
            self.act(var[:, 0:tw], s1[:, 0:tw], AF.Square, scale=1.0 / 64, r=[s1T], w=[varT])
            self.stt(var[:, 0:tw], s2[:, 0:tw], 1.0 / 64, var[:, 0:tw], ALU.mult, ALU.subtract, r=[s2T, varT], w=[varT])
            self.rstd_from(var[:, 0:tw], var[:, 0:tw], 1.0, self.EPS_GN, r=[varT], w=[varT])
            dd, ddT = Osq, OsqT
            self.stt(dd[:, 0:tw], s1[:, 0:tw], -1.0 / 64, Osb[:, 0:tw], ALU.mult, ALU.add, r=[s1T, OsbT], w=[ddT])
            self.tt(dd[:, 0:tw], dd[:, 0:tw], var[:, 0:tw], ALU.mult, r=[ddT, varT], w=[ddT])
            self.stt(dd[:, 0:tw], dd[:, 0:tw], self.pvc(l, P_LG + j), bonb[s][:, 0:tw], ALU.mult, ALU.add,
                     r=[ddT, bonT[s], self.cT], w=[ddT])
            self.stt(yT[:, 0, t0:t0 + tw], dd[:, 0:tw], self.pvc(l, P_LB + j), gbufb[s][:, 0:tw], ALU.add, ALU.mult,
                     r=[ddT, gbT[s], self.cT], w=[yTT[0][ti]])
            yield

        def drain(g):
            for _ in g:
                pass

        for j in range(3):
            for q, c0 in enumerate((512, 896, 1280)):
                self.ld(wr[:, :, q, :], w_in[l, :, c0 + j * 128:c0 + (j + 1) * 128].rearrange("(c p) f -> p c f", p=128),
                        w=[wrT], q="pool")
            S0, S0T = Sbr.next()
            self.S.op("dve", lambda e, S0=S0: e.memset(S0[:, :], 0.0), writes=[S0T])
            state["S"] = (S0, S0T)
            drain(prep(j, 0))
            for ti in range(len(TILES)):
                sg_ = seq(j, ti)
                pg_ = prep(j, ti + 1) if ti + 1 < len(TILES) else None
                nsteps = TILES[ti][1] // 64 + 1
                per = -(-18 // nsteps)
                alive_s, alive_p = True, pg_ is not None
                while alive_s or alive_p:
                    if alive_s:
                        try:
                            next(sg_)
                        except StopIteration:
                            alive_s = False
                    if alive_p:
                        for _ in range(per):
                            try:
                                next(pg_)
                            except StopIteration:
                                alive_p = False
                                break
            self.wout(l, 256 + j * 128, 1, yT, yTT, wo, woT)

    def sb_branch(self, l):
        self.aoff = 0
        w_in = self.d["w_in"]
        yT = self.aalloc([1, LP], BF16)
        wo = self.aalloc([1, D], BF16)
        woT = T()
        wq = self.aalloc([8, 384], BF16); wqT = T()
        wk = self.aalloc([8, 384], BF16); wkT = T()
        wv = self.aalloc([8, 384], BF16); wvT = T()
        for wa, wt, c0 in ((wq, wqT, 1920), (wk, wkT, 2304), (wv, wvT, 2688)):
            self.ld(wa, w_in[l, :, c0:c0 + 384].rearrange("(c p) f -> p c f", p=128), w=[wt], q="pool")
        qT = self.aalloc([3, LP], BF16)
        kT = self.aalloc([3, LP], BF16)
        vtm = self.aalloc([17, 384], BF16)
        qTT = [T() for _ in TILES]
        kTT = [T() for _ in range(17)]
        vT = [T() for _ in range(17)]
        Er = self.ring(2, [512], F32)
        spr = self.ring(3, [512], BF16)
        Ar = self.ring(3, [512], BF16)
        Rr = self.ring(3, [512], BF16)
        qz = [[self.aalloc([512], BF16) for _ in range(2)] for _ in range(2)]
        qzT = [[T() for _ in range(2)] for _ in range(2)]
        Osb = self.aalloc([512], F32); OsbT = T()
        sqb = self.aalloc([512], F32); sqT = T()
        rs = self.aalloc([512], F32); rsT = T()
        for s_ in range(2):
            self.S.op("dve", lambda e, s_=s_: e.memset(qz[s_][0][64:128, :], 0.0), writes=[qzT[s_][0]])
            self.S.op("dve", lambda e, s_=s_: e.memset(qz[s_][1][0:64, :], 0.0), writes=[qzT[s_][1]])
        for ti, (t0, tw) in enumerate(TILES):
            for j in range(3):
                pq, pqT = self.nb()
                for c in range(8):
                    self.mm(pq[:, 0:tw], wq[:, c, j * 128:(j + 1) * 128], self.XN[:, c, t0:t0 + tw],
                            start=(c == 0), stop=(c == 7), r=[wqT, self.XNT[ti]], w=[pqT])
                self.act(qT[:, j, t0:t0 + tw], pq[:, 0:tw], AF.Copy, scale=0.125, r=[pqT], w=[qTT[ti]])
                pk, pkT = self.nb()
                for c in range(8):
                    self.mm(pk[:, 0:tw], wk[:, c, j * 128:(j + 1) * 128], self.XN[:, c, t0:t0 + tw],
                            start=(c == 0), stop=(c == 7), r=[wkT, self.XNT[ti]], w=[pkT])
                kts = [kTT[b] for b in range(t0 // 128, (t0 + tw) // 128)]
                self.cp(kT[:, j, t0:t0 + tw], pk[:, 0:tw], r=[pkT], w=kts)
        for blk in range(17):
            ti = min(blk // 4, 4)
            pv_, pvT = self.nb()
            for c in range(8):
                self.mm(pv_[:, 0:384], self.XN[:, c, blk * 128:(blk + 1) * 128], wv[:, c, :],
                        start=(c == 0), stop=(c == 7), r=[wvT, self.XNT[ti]], w=[pvT])
            self.cp(vtm[:, blk, :], pv_[:, 0:384], r=[pvT], w=[vT[blk]])
        ntri = self.CB[:, C_NTRI:C_NTRI + 128]
        nones = self.CB[:, C_NONES:C_NONES + 128]
        bonesF = self.CF[:, C_BONES:C_BONES + 128]
        one = self.PD[:, self.ONE:self.ONE + 1]
        blocks = []
        g = 0
        for j in range(3):
            for ti, (t0, tw) in enumerate(TILES):
                nkb = (t0 + tw) // 128
                for e in range(2):
                    for kb in reversed(range(nkb)):
                        blocks.append(dict(j=j, ti=ti, e=e, kb=kb, nkb=nkb, g=g))
                g += 1
        zi = [0]
        chain = {}
        yTT1 = [[T() for _ in TILES]]
        yTTs = {j: yTT1 for j in range(3)}

        def s0(b):
            j, ti, e, kb, g_ = b["j"], b["ti"], b["e"], b["kb"], b["g"]
            t0, tw = TILES[ti]
            st = g_ % 2
            if e == 0 and kb == b["nkb"] - 1:
                self.cp(qz[st][0][0:64, 0:tw], qT[0:64, j, t0:t0 + tw], r=[qTT[ti]], w=[qzT[st][0]])
                self.cp(qz[st][1][64:128, 0:tw], qT[64:128, j, t0:t0 + tw], r=[qTT[ti]], w=[qzT[st][1]])
            Z, ZT = self.pb[zi[0] % 3], self.pbT[zi[0] % 3]
            zi[0] += 1
            b["Z"], b["ZT"] = Z, ZT
            self.mm(Z[:, 0:tw], kT[:, j, kb * 128:(kb + 1) * 128], qz[st][e][:, 0:tw], start=True, stop=False,
                    r=[kTT[kb], qzT[st][e]], w=[ZT], sgc=True)
            Ea, ET = Er.next()
            self.act(Ea[:, 0:tw], Z[:, 0:tw], AF.Exp, r=[ZT], w=[ET])
            sa, sT = spr.next()
            b["sp"], b["spT"] = sa, sT
            self.act(sa[:, 0:tw], Ea[:, 0:tw], AF.Ln, bias=one, r=[ET, self.cT], w=[sT])
            off = kb * 128 - t0
            b["mask"] = None
            if off >= 0:
                mask = self.CB[:, C_SBM + (off // 128) * 512:C_SBM + (off // 128) * 512 + tw]
                b["mask"] = mask
                self.tt(sa[:, 0:tw], sa[:, 0:tw], mask, ALU.mult, r=[sT, self.cT], w=[sT])

        def s1(b):
            j, ti, e, kb = b["j"], b["ti"], b["e"], b["kb"]
            t0, tw = TILES[ti]
            first = kb == b["nkb"] - 1
            Z, ZT, sa, sT = b["Z"], b["ZT"], b["sp"], b["spT"]
            key = (j, ti, e)
            self.mm(Z[:, 0:tw], ntri, sa[:, 0:tw], start=False, stop=first, r=[sT, self.cT], w=[ZT], sgc=True)
            if not first:
                Rb, RbT = chain[key]
                self.mm(Z[:, 0:tw], nones, Rb[:, 0:tw], start=False, stop=True, r=[RbT, self.cT], w=[ZT], sgc=True)
            Aa, AT_ = Ar.next()
            b["A"], b["AT"] = Aa, AT_
            self.act(Aa[:, 0:tw], Z[:, 0:tw], AF.Exp, r=[ZT], w=[AT_])
            if b["mask"] is not None:
                self.tt(Aa[:, 0:tw], Aa[:, 0:tw], b["mask"], ALU.mult, r=[AT_, self.cT], w=[AT_])
            if kb > 0:
                Rn, RnT = Rr.next()
                if first:
                    self.cp(Rn[:, 0:tw], sa[:, 0:tw], r=[sT], w=[RnT])
                else:
                    self.tt(Rn[:, 0:tw], Rb[:, 0:tw], sa[:, 0:tw], ALU.add, r=[RbT, sT], w=[RnT])
                chain[key] = (Rn, RnT)

        def s2(b):
            j, ti, e, kb, g_ = b["j"], b["ti"], b["e"], b["kb"], b["g"]
            t0, tw = TILES[ti]
            first = kb == b["nkb"] - 1
            ob = 3
            O, OT = self.pb[ob + e], self.pbT[ob + e]
            self.mm(O[:, 0:tw], vtm[:, kb, j * 128:(j + 1) * 128], b["A"][:, 0:tw], start=first, stop=(kb == 0),
                    r=[vT[kb], b["AT"]], w=[OT])
            if e == 1 and kb == 0:
                for ee in range(2):
                    rows = slice(ee * 64, ee * 64 + 64)
                    self.cp(Osb[rows, 0:tw], self.pb[ob + ee][rows, 0:tw], r=[self.pbT[ob + ee]], w=[OsbT])
                self.tt(sqb[:, 0:tw], Osb[:, 0:tw], Osb[:, 0:tw], ALU.mult, r=[OsbT], w=[sqT])
                ss, ssT = self.pb[5], self.pbT[5]
                self.mm(ss[:, 0:tw], bonesF, sqb[:, 0:tw], r=[sqT, self.cT], w=[ssT])
                self.rstd_from(rs[:, 0:tw], ss[:, 0:tw], 1.0 / 64, self.EPS_RMS, r=[ssT], w=[rsT])
                self.stt(yT[:, 0, t0:t0 + tw], Osb[:, 0:tw], self.pvc(l, P_SBN + j), rs[:, 0:tw], ALU.mult, ALU.mult,
                         r=[OsbT, rsT, self.cT], w=[yTTs[j][0][ti]])
                if ti == len(TILES) - 1:
                    self.wout(l, 640 + j * 128, 1, yT, yTTs[j], wo, woT, banks=[6, 7])

        n = len(blocks)
        for i in range(n + 2):
            if i < n:
                s0(blocks[i])
            if 0 <= i - 1 < n:
                s1(blocks[i - 1])
            if 0 <= i - 2 < n:
                s2(blocks[i - 2])


def pack_params(inp, nl):
    def col(v):
        v = np.asarray(v, np.float32).reshape(-1)
        return v.reshape(-1, 128).T

    pv = np.zeros((128, nl * NPV + 8), np.float32)
    for l in range(nl):
        o = l * NPV
        pv[:, o + P_F1N:o + P_F1N + 8] = col(inp["ffn1_norm"][l])
        pv[:, o + P_MXN:o + P_MXN + 8] = col(inp["mix_norm"][l])
        pv[:, o + P_F2N:o + P_F2N + 8] = col(inp["ffn2_norm"][l])
        cw = np.asarray(inp["conv_w"][l], np.float32)
        for cc in range(2):
            pv[:, o + P_CW + cc * 31:o + P_CW + (cc + 1) * 31] = cw[:, cc * 128:(cc + 1) * 128].T
        pv[:, o + P_CB:o + P_CB + 2] = col(inp["conv_b"][l])
        pv[:, o + P_CLG:o + P_CLG + 2] = col(inp["conv_ln_g"][l])
        pv[:, o + P_CLB:o + P_CLB + 2] = col(inp["conv_ln_b"][l])
        pv[:, o + P_MU:o + P_MU + 11] = col(inp["rwkv_mu"][l])
        pv[:, o + P_W0:o + P_W0 + 3] = col(inp["rwkv_w0"][l])
        pv[:, o + P_A0:o + P_A0 + 3] = col(inp["rwkv_a0"][l])
        pv[:, o + P_KK:o + P_KK + 3] = col(inp["rwkv_kk"][l])
        pv[:, o + P_KA:o + P_KA + 3] = col(inp["rwkv_ka"][l])
        pv[:, o + P_RK:o + P_RK + 3] = col(inp["rwkv_rk"][l])
        pv[:, o + P_LG:o + P_LG + 3] = col(inp["rwkv_ln_g"][l])
        pv[:, o + P_LB:o + P_LB + 3] = col(inp["rwkv_ln_b"][l])
        pv[:, o + P_SBN:o + P_SBN + 3] = col(inp["sb_norm"][l])
    pv[:, nl * NPV:nl * NPV + 8] = col(inp["final_norm"])
    return pv


def make_consts():
    c = np.zeros((128, NCST), np.float32)
    i = np.arange(128)[:, None]
    j = np.arange(128)[None, :]
    c[:, C_ID:C_ID + 128] = (i == j)
    c[:, C_ONES:C_ONES + 128] = 1.0
    c[:, C_BONES:C_BONES + 128] = ((i // 64) == (j // 64))
    c[:, C_NTRI:C_NTRI + 128] = -(i >= j).astype(np.float32)
    c[:, C_NONES:C_NONES + 128] = -1.0
    u = np.arange(512)[None, :]
    for jj in range(4):
        c[:, C_SBM + jj * 512:C_SBM + (jj + 1) * 512] = (u > 128 * jj + i)
    s = (np.arange(128) % 64)[:, None]
    t = np.arange(64)[None, :]
    c[:, C_CMU:C_CMU + 64] = (s < t)
    c[:, C_CMU + 64:C_CMU + 128] = (s <= t)
    c[:, C_CML:C_CML + 64] = (s > t)
    c[:, C_I64:C_I64 + 64] = (s == t)
    return c


_CACHE = {}


def run(inputs, nl=NL_FULL, do_mixer=True, dbg=None, branches=("conv", "rwkv", "sb")):
    key = (nl, do_mixer, dbg, tuple(branches))
    if key not in _CACHE:
        _CACHE[key] = Builder(nl, do_mixer, dbg, tuple(branches)).build()
    nc = _CACHE[key]
    x = np.asarray(inputs["x"], np.float32)
    meta = np.asarray(inputs["meta"], np.float32)
    B = x.shape[0]
    pv = pack_params(inputs, nl)
    cst = make_consts()
    shared = {"pvec": pv, "cst": cst}
    for k in ("ffn1_w13", "ffn1_w2", "ffn2_w13", "ffn2_w2", "w_in", "w_out", "rwkv_wB", "rwkv_aB", "rwkv_gB"):
        shared[k] = np.ascontiguousarray(np.asarray(inputs[k], np.float32)[:nl])
    in_maps = []
    for b in range(B):
        xp = np.zeros((LP, D), np.float32)
        xp[:NMETA] = meta
        xp[NMETA:NMETA + SEQ] = x[b]
        m = dict(shared)
        m["xpad"] = xp
        in_maps.append(m)
    res = run_bass_kernel_spmd(nc, in_maps, core_ids=list(range(B)))
    out = np.stack([r["out"] for r in res.results], axis=0)
    if dbg:
        return out, [r["dbg"] for r in res.results]
    return out


def kernel(**inputs):
    return run(inputs)
```

```python
import numpy as np
import concourse.bass as bass
import concourse.mybir as mybir
from concourse.bass_utils import run_bass_kernel_spmd

F32 = mybir.dt.float32
BF16 = mybir.dt.bfloat16
AF = mybir.ActivationFunctionType
ALU = mybir.AluOpType
NDS = 24

NL_FULL = 4
D = 1024
NMETA = 16
SEQ = 2048
LP = 2176
TILES = [(0, 512), (512, 512), (1024, 512), (1536, 512), (2048, 128)]
GROUPS = [[0, 1], [2, 3, 4]]
DFF = 2816
NFC = 22
NPV = 128
P_F1N, P_MXN, P_F2N, P_CW, P_CB, P_CLG, P_CLB = 0, 8, 16, 24, 86, 88, 90
P_MU, P_W0, P_A0, P_KK, P_KA, P_RK, P_LG, P_LB, P_SBN = 92, 103, 106, 109, 112, 115, 118, 121, 124
C_ID, C_ONES, C_BONES, C_NTRI, C_NONES, C_SBM, C_CMU, C_CML, C_I64 = 0, 128, 256, 384, 512, 640, 2688, 2816, 2880
NCST = 2944
ARENA_WORDS = 23296
RMS_EPS = 1e-6
LN_EPS = 1e-5
GN_EPS = 64e-5
DECAY_C = 0.6065306597126334


class T:
    __slots__ = ("w", "r", "excl")

    def __init__(self, excl=False):
        self.w = None
        self.r = {}
        self.excl = excl


class Sched:
    ENG = ["pe", "act", "dve", "pool", "sp"]

    def __init__(self, nc):
        self.nc = nc
        self.sem = {e: nc.alloc_semaphore(name=f"s_{e}") for e in self.ENG}
        self.cnt = {e: 0 for e in self.ENG}
        self.prog = {e: [] for e in self.ENG}
        self.seen = {e: {} for e in self.ENG}
        self.dsems = [nc.alloc_semaphore(name=f"dq{i}") for i in range(NDS)]
        self.dcnt = [0] * NDS
        self.dnext = {"pool": 0, "sp": 0}

    def _deps(self, e, reads, writes, need):
        def req(ev, raw):
            if ev is None:
                return
            key, sem, val = ev
            if key == e and e == "pe":
                return
            if key not in need or need[key][1] < val:
                need[key] = (sem, val)

        for t in reads:
            req(t.w, True)
            if t.excl:
                for k_, ev in t.r.items():
                    if k_ != e:
                        req(ev, False)
        for t in writes:
            req(t.w, False)
            for ev in t.r.values():
                req(ev, False)
        waits = []
        for key, (sem, val) in need.items():
            if self.seen[e].get(key, 0) < val:
                self.seen[e][key] = val
                waits.append((sem, val))
        return waits

    def op(self, e, fn, reads=(), writes=()):
        waits = self._deps(e, reads, writes, {})
        self.cnt[e] += 1
        ev = (e, self.sem[e], self.cnt[e])
        self.prog[e].append((waits, fn, (self.sem[e], 1)))
        for t in reads:
            t.r[e] = ev
        for t in writes:
            t.w = ev
            t.r = {}
        return ev

    def dma(self, q, fn, reads=(), writes=()):
        half = NDS // 2
        k_ = self.dnext[q]
        self.dnext[q] = (k_ + 1) % half
        i = k_ + (half if q == "pool" else 0)
        sem = self.dsems[i]
        key = ("d", i)
        need = {}
        if self.dcnt[i] > 0:
            need[key] = (sem, self.dcnt[i] * 16)
        self.dcnt[i] += 1
        val = self.dcnt[i] * 16
        waits = self._deps(q, reads, writes, need)
        self.prog[q].append((waits, fn, (sem, 16)))
        ev = (key, sem, val)
        for t in reads:
            t.r[key] = ev
        for t in writes:
            t.w = ev
            t.r = {}
        return ev

    def wait_events(self, e, evs):
        waits = []
        for key, sem, val in evs:
            if self.seen[e].get(key, 0) < val:
                self.seen[e][key] = val
                waits.append((sem, val))
        self.prog[e].append((waits, None, None))

    def fence(self, e):
        if self.cnt[e] > self.seen[e].get(e, 0):
            self.seen[e][e] = self.cnt[e]
            self.prog[e].append(([(self.sem[e], self.cnt[e])], None, None))

    def barrier(self):
        evs = [(e, self.sem[e], self.cnt[e]) for e in self.ENG if self.cnt[e] > 0]
        evs += [(("d", i), self.dsems[i], self.dcnt[i] * 16) for i in range(NDS) if self.dcnt[i] > 0]
        for e in self.ENG:
            self.wait_events(e, [ev for ev in evs if ev[0] != e])

    def emit(self):
        with self.nc.Block() as block:
            decs = {"pe": block.tensor, "act": block.scalar, "dve": block.vector,
                    "pool": block.gpsimd, "sp": block.sync}
            for name in self.ENG:
                prog = self.prog[name]

                def body(eng, prog=prog):
                    for waits, fn, inc in prog:
                        for sem, val in waits:
                            eng.wait_ge(sem, val)
                        if fn is not None:
                            ins = fn(eng)
                            if inc is not None:
                                ins.then_inc(inc[0], inc[1])

                decs[name](body)


class Ring:
    def __init__(self, aps):
        self.aps = aps
        self.ts = [T() for _ in aps]
        self.i = 0

    def next(self):
        i = self.i
        self.i = (i + 1) % len(self.aps)
        return self.aps[i], self.ts[i]


class Builder:
    def __init__(self, nl, do_mixer=True, dbg=None, branches=("conv", "rwkv", "sb")):
        self.nl = nl
        self.branches = branches
        self.dbg = dbg
        self.do_mixer = do_mixer
        nc = bass.Bass("TRN2", target_bir_lowering=False)
        self.nc = nc
        self.S = Sched(nc)
        d = {}

        def din(name, shape):
            d[name] = nc.dram_tensor(name, shape, F32, kind="ExternalInput").ap()

        din("xpad", [LP, D])
        din("pvec", [128, nl * NPV + 8])
        din("cst", [128, NCST])
        for w in ("ffn1", "ffn2"):
            din(f"{w}_w13", [nl, D, 2 * DFF])
            din(f"{w}_w2", [nl, DFF, D])
        din("w_in", [nl, D, 3072])
        din("w_out", [nl, D, D])
        din("rwkv_wB", [nl, 64, 384])
        din("rwkv_aB", [nl, 64, 384])
        din("rwkv_gB", [nl, 128, 384])
        self.d = d
        self.out = nc.dram_tensor("out", [SEQ, D], F32, kind="ExternalOutput").ap()
        if dbg:
            self.dbg_out = nc.dram_tensor("dbg", [128, 8, LP], F32, kind="ExternalOutput").ap()
        self.H = nc.alloc_sbuf_tensor("H", [128, 8, LP], F32)
        self.XN = nc.alloc_sbuf_tensor("XN", [128, 8, LP], BF16)
        self.CF = nc.alloc_sbuf_tensor("CF", [128, 384], F32)
        self.CB = nc.alloc_sbuf_tensor("CB", [128, NCST], BF16)
        self.PV = nc.alloc_sbuf_tensor("PV", [128, nl * NPV + 8], F32)
        self.PD = nc.alloc_sbuf_tensor("PD", [128, nl * 8 + 8], F32)
        self.ARENA = nc.alloc_sbuf_tensor("ARENA", [128, ARENA_WORDS], F32)
        self.pb = [nc.alloc_psum_tensor(f"pb{i}", [128, 512], F32) for i in range(8)]
        self.pbT = [T(excl=True) for _ in range(8)]
        self.HT = [T() for _ in TILES]
        self.XNT = [T() for _ in TILES]
        self.cT = T()
        self.aoff = 0

    def aalloc(self, shape, dt):
        n = int(np.prod(shape))
        words = (n * (4 if dt == F32 else 2) + 3) // 4
        ap = self.ARENA[:, self.aoff:self.aoff + words]
        self.aoff += words
        assert self.aoff <= ARENA_WORDS, (self.aoff, ARENA_WORDS)
        if dt == BF16:
            ap = ap.bitcast(BF16)
        if len(shape) == 2:
            ap = ap.rearrange("p (a b) -> p a b", b=shape[1])
        elif len(shape) == 3:
            ap = ap.rearrange("p (a b c) -> p a b c", b=shape[1], c=shape[2])
        return ap

    def ring(self, n, shape, dt):
        return Ring([self.aalloc(shape, dt) for _ in range(n)])

    def mm(self, out, lhsT, rhs, start=True, stop=True, r=(), w=(), sgc=False):
        return self.S.op("pe", lambda e: e.matmul(out, lhsT, rhs, start=start, stop=stop, skip_group_check=sgc),
                         reads=r, writes=w)

    def tr(self, out, in_, ident, r=(), w=()):
        return self.S.op("pe", lambda e: e.transpose(out, in_, ident), reads=r, writes=w)

    def act(self, out, in_, func, bias=None, scale=None, r=(), w=()):
        kw = {}
        if bias is not None:
            kw["bias"] = bias
        if scale is not None:
            kw["scale"] = scale
        return self.S.op("act", lambda e: e.activation(out=out, in_=in_, func=func, **kw), reads=r, writes=w)

    def tt(self, out, in0, in1, op, r=(), w=(), eng="dve"):
        return self.S.op(eng, lambda e: e.tensor_tensor(out=out, in0=in0, in1=in1, op=op), reads=r, writes=w)

    def ts(self, out, in0, s1, s2, op0, op1=None, r=(), w=(), eng="dve"):
        if op1 is None:
            return self.S.op(eng, lambda e: e.tensor_scalar(out=out, in0=in0, scalar1=s1, scalar2=None, op0=op0),
                             reads=r, writes=w)
        return self.S.op(eng, lambda e: e.tensor_scalar(out=out, in0=in0, scalar1=s1, scalar2=s2, op0=op0, op1=op1),
                         reads=r, writes=w)

    def stt(self, out, in0, scalar, in1, op0, op1, r=(), w=()):
        return self.S.op("dve", lambda e: e.scalar_tensor_tensor(out=out, in0=in0, scalar=scalar, in1=in1,
                                                                 op0=op0, op1=op1), reads=r, writes=w)

    def cp(self, out, in_, r=(), w=(), eng="dve"):
        return self.S.op(eng, lambda e: e.tensor_copy(out=out, in_=in_), reads=r, writes=w)

    def ld(self, out, in_, w=(), q="sp", r=()):
        return self.S.dma(q, lambda e: e.dma_start(out=out, in_=in_), reads=r, writes=w)

    def pvc(self, l, col, n=1):
        c = l * NPV + col
        return self.PV[:, c:c + n]

    def rstd_from(self, out, ps_in, scale, eps_col, r, w):
        self.act(out, ps_in, AF.Ln, bias=self.PD[:, eps_col:eps_col + 1], scale=scale, r=list(r) + [self.cT], w=w)
        self.act(out, out, AF.Exp, scale=-0.5, r=w, w=w)

    def prologue(self):
        nl = self.nl
        self.ld(self.PV[:], self.d["pvec"], w=[self.cT])
        self.ld(self.CF[:], self.d["cst"][:, 0:384], w=[self.cT])
        self.ld(self.CB[:], self.d["cst"], w=[self.cT], q="pool")
        e0 = nl * 8
        self.S.op("dve", lambda e: e.memset(self.PD[:, e0 + 0:e0 + 1], RMS_EPS), writes=[self.cT])
        self.S.op("dve", lambda e: e.memset(self.PD[:, e0 + 1:e0 + 2], LN_EPS), writes=[self.cT])
        self.S.op("dve", lambda e: e.memset(self.PD[:, e0 + 2:e0 + 3], GN_EPS), writes=[self.cT])
        self.S.op("dve", lambda e: e.memset(self.PD[:, e0 + 3:e0 + 4], 1.0), writes=[self.cT])
        self.S.op("dve", lambda e: e.memset(self.PD[:, e0 + 4:e0 + 5], 0.0), writes=[self.cT])
        for l in range(nl):
            self.ts(self.PD[:, l * 8:l * 8 + 3], self.pvc(l, P_KA, 3), -1.0, 1.0, ALU.mult, ALU.add,
                    r=[self.cT], w=[self.cT])
        self.EPS_RMS, self.EPS_LN, self.EPS_GN, self.ONE, self.ZERO = e0, e0 + 1, e0 + 2, e0 + 3, e0 + 4
        self.aoff = 0
        xr = self.ring(3, [D], F32)
        ident = self.CF[:, 0:128]
        k = 0
        for blk in range(LP // 128):
            xa, xt = xr.next()
            self.ld(xa, self.d["xpad"][blk * 128:(blk + 1) * 128, :], w=[xt])
            ti = next(i for i, (t0, tw) in enumerate(TILES) if t0 <= blk * 128 < t0 + tw)
            for half in range(2):
                pb, pt = self.pb[k % 8], self.pbT[k % 8]
                k += 1
                for c in range(4):
                    cc = half * 4 + c
                    self.tr(pb[:, c * 128:(c + 1) * 128], xa[:, cc * 128:(cc + 1) * 128], ident,
                            r=[xt, self.cT], w=[pt])
                self.cp(self.H[:, half * 4:half * 4 + 4, blk * 128:(blk + 1) * 128],
                        pb[:, :].rearrange("p (a b) -> p a b", b=128), r=[pt], w=[self.HT[ti]],
                        eng=("dve" if half == 0 else "act") if False else "dve")
        self.S.barrier()

    def rmsnorm_tile(self, ti, gcol_l, sqring, rstd, rstdT, xn_dst, xn_T, tok_off=0):
        t0, tw = TILES[ti]
        sp, spT = self.pb[6 + (ti % 2)], self.pbT[6 + (ti % 2)]
        ones = self.CB[:, C_ONES:C_ONES + 128]
        for c in range(8):
            sq, sqT = sqring.next()
            self.act(sq[:, 0:tw], self.H[:, c, t0:t0 + tw], AF.Square, r=[self.HT[ti]], w=[sqT])
            self.mm(sp[:, 0:tw], ones, sq[:, 0:tw], start=(c == 0), stop=(c == 7), r=[sqT, self.cT], w=[spT])
        self.rstd_from(rstd[:, 0:tw], sp[:, 0:tw], 1.0 / D, self.EPS_RMS, r=[spT], w=[rstdT])
        for c in range(8):
            self.stt(xn_dst[:, c, tok_off:tok_off + tw], self.H[:, c, t0:t0 + tw], gcol_l[:, c:c + 1],
                     rstd[:, 0:tw], ALU.mult, ALU.mult, r=[self.HT[ti], rstdT, self.cT], w=[xn_T])

    def setup_ffn_arena(self):
        self.aoff = 0
        A = {}
        A["g"] = self.aalloc([NFC, 1152], BF16)
        A["gT"] = [T() for _ in range(3)]
        A["w13"] = self.ring(2, [8, 2, 256], BF16)
        A["w2"] = self.ring(2, [NFC, 128], BF16)
        A["sq"] = self.ring(4, [512], BF16)
        A["rstd"] = self.aalloc([512], F32)
        A["rstdT"] = T()
        A["sg"] = self.ring(2, [512], F32)
        self.FA = A

    def ffn(self, l, which):
        A = self.FA
        w13 = self.d[f"ffn{which}_w13"]
        w2 = self.d[f"ffn{which}_w2"]
        gcol = self.pvc(l, P_F1N if which == 1 else P_F2N, 8)
        for ti in range(len(TILES)):
            self.rmsnorm_tile(ti, gcol, A["sq"], A["rstd"], A["rstdT"], self.XN, self.XNT[ti], tok_off=TILES[ti][0])
        for grp in GROUPS:
            g0 = TILES[grp[0]][0]
            for fp in range(NFC // 2):
                wa, wt = A["w13"].next()
                f0 = fp * 256
                self.ld(wa[:, :, 0, :], w13[l, :, f0:f0 + 256].rearrange("(c p) f -> p c f", p=128), w=[wt], q="pool")
                self.ld(wa[:, :, 1, :], w13[l, :, DFF + f0:DFF + f0 + 256].rearrange("(c p) f -> p c f", p=128),
                        w=[wt], q="pool")
                for fi in range(2):
                    fc = fp * 2 + fi
                    for j, ti in enumerate(grp):
                        t0, tw = TILES[ti]
                        pg, pgT = self.pb[j % 2], self.pbT[j % 2]
                        pu, puT = self.pb[2 + j % 2], self.pbT[2 + j % 2]
                        for c in range(8):
                            self.mm(pg[:, 0:tw], wa[:, c, 0, fi * 128:(fi + 1) * 128], self.XN[:, c, t0:t0 + tw],
                                    start=(c == 0), stop=(c == 7), r=[wt, self.XNT[ti]], w=[pgT])
                        for c in range(8):
                            self.mm(pu[:, 0:tw], wa[:, c, 1, fi * 128:(fi + 1) * 128], self.XN[:, c, t0:t0 + tw],
                                    start=(c == 0), stop=(c == 7), r=[wt, self.XNT[ti]], w=[puT])
                        sg, sgT = A["sg"].next()
                        self.act(sg[:, 0:tw], pg[:, 0:tw], AF.Silu, r=[pgT], w=[sgT])
                        self.tt(A["g"][:, fc, t0 - g0:t0 - g0 + tw], pu[:, 0:tw], sg[:, 0:tw], ALU.mult,
                                r=[puT, sgT], w=[A["gT"][j]])
            for dc in range(8):
                wa, wt = A["w2"].next()
                self.ld(wa, w2[l, :, dc * 128:(dc + 1) * 128].rearrange("(c p) f -> p c f", p=128), w=[wt], q="pool")
                for j, ti in enumerate(grp):
                    t0, tw = TILES[ti]
                    po, poT = self.pb[4 + j % 2], self.pbT[4 + j % 2]
                    for fc in range(NFC):
                        self.mm(po[:, 0:tw], wa[:, fc, :], A["g"][:, fc, t0 - g0:t0 - g0 + tw],
                                start=(fc == 0), stop=(fc == NFC - 1), r=[wt, A["gT"][j]], w=[poT])
                    self.stt(self.H[:, dc, t0:t0 + tw], po[:, 0:tw], 0.5, self.H[:, dc, t0:t0 + tw],
                             ALU.mult, ALU.add, r=[poT, self.HT[ti]], w=[self.HT[ti]])

    def epilogue(self):
        nl = self.nl
        self.S.barrier()
        self.aoff = 0
        sqr = self.ring(4, [512], BF16)
        rstd = self.aalloc([512], F32)
        rstdT = T()
        yn = self.ring(2, [8, 512], F32)
        ob = self.ring(3, [D], F32)
        gcol = self.PV[:, nl * NPV:nl * NPV + 8]
        ones = self.CB[:, C_ONES:C_ONES + 128]
        ident = self.CF[:, 0:128]
        evs = []
        k = 0
        for ti, (t0, tw) in enumerate(TILES):
            ya, yt = yn.next()
            sp, spT = self.pb[6 + (ti % 2)], self.pbT[6 + (ti % 2)]
            for c in range(8):
                sq, sqT = sqr.next()
                self.act(sq[:, 0:tw], self.H[:, c, t0:t0 + tw], AF.Square, r=[self.HT[ti]], w=[sqT])
                self.mm(sp[:, 0:tw], ones, sq[:, 0:tw], start=(c == 0), stop=(c == 7), r=[sqT, self.cT], w=[spT])
            self.rstd_from(rstd[:, 0:tw], sp[:, 0:tw], 1.0 / D, self.EPS_RMS, r=[spT], w=[rstdT])
            for c in range(8):
                self.stt(ya[:, c, 0:tw], self.H[:, c, t0:t0 + tw], gcol[:, c:c + 1], rstd[:, 0:tw],
                         ALU.mult, ALU.mult, r=[self.HT[ti], rstdT, self.cT], w=[yt])
            for b in range(tw // 128):
                tok = t0 + b * 128
                oa, ot = ob.next()
                for half in range(2):
                    pb, pt = self.pb[k % 6], self.pbT[k % 6]
                    k += 1
                    for c in range(4):
                        cc = half * 4 + c
                        self.tr(pb[:, c * 128:(c + 1) * 128], ya[:, cc, b * 128:(b + 1) * 128], ident,
                                r=[yt, self.cT], w=[pt])
                    self.cp(oa[:, half * 512:(half + 1) * 512], pb[:, :], r=[pt], w=[ot])
                lo = max(tok, NMETA)
                hi = min(tok + 128, NMETA + SEQ)
                if hi > lo:
                    evs.append(self.ld(self.out[lo - NMETA:hi - NMETA, :], oa[lo - tok:hi - tok, :], r=[ot]))
        self.S.wait_events("sp", evs)

    def dump(self, src_fn):
        evs = []
        self.S.barrier()
        for c in range(8):
            evs.append(self.ld(self.dbg_out[:, c, :], src_fn(c)))
        self.S.wait_events("sp", evs)
        self.S.barrier()

    def build(self):
        self.prologue()
        self.setup_ffn_arena()
        for l in range(self.nl):
            self.ffn(l, 1)
            if self.do_mixer:
                self.S.barrier()
                self.mixer(l)
                self.S.barrier()
            self.ffn(l, 2)
        if self.dbg == "h":
            self.dump(lambda c: self.H[:, c, :])
        self.epilogue()
        self.S.emit()
        return self.nc


    def nb(self):
        i = self._nb % self._nbn
        self._nb = i + 1
        return self.pb[i], self.pbT[i]

    def wout(self, l, row0, nch, yT, yTT, wo, woT, banks=None):
        w_out = self.d["w_out"]
        kk = [0]

        def nbk():
            if banks is None:
                return self.nb()
            b_ = banks[kk[0] % len(banks)]
            kk[0] += 1
            return self.pb[b_], self.pbT[b_]

        self.ld(wo[:, 0:nch, :], w_out[l, row0:row0 + nch * 128, :].rearrange("(c p) f -> p c f", p=128),
                w=[woT], q="pool")
        for dc in range(8):
            for ti, (t0, tw) in enumerate(TILES):
                po, poT = nbk()
                for c in range(nch):
                    self.mm(po[:, 0:tw], wo[:, c, dc * 128:(dc + 1) * 128], yT[:, c, t0:t0 + tw],
                            start=(c == 0), stop=(c == nch - 1), r=[woT, yTT[c][ti]], w=[poT])
                self.tt(self.H[:, dc, t0:t0 + tw], po[:, 0:tw], self.H[:, dc, t0:t0 + tw], ALU.add,
                        r=[poT, self.HT[ti]], w=[self.HT[ti]])

    def mixer(self, l):
        self._nb = 0
        self._nbn = 7
        self.aoff = 0
        sq = self.ring(4, [512], BF16)
        rstd = self.aalloc([512], F32)
        rstdT = T()
        gcol = self.pvc(l, P_MXN, 8)
        for ti in range(len(TILES)):
            self.rmsnorm_tile(ti, gcol, sq, rstd, rstdT, self.XN, self.XNT[ti], tok_off=TILES[ti][0])
        br = self.branches
        if "conv" in br:
            self.S.barrier()
            self.conv_branch(l)
        if "rwkv" in br:
            self.S.barrier()
            self.rwkv_branch(l)
        if "sb" in br:
            self.S.barrier()
            self.sb_branch(l)

    def conv_branch(self, l):
        self.aoff = 0
        w_in = self.d["w_in"]
        yT = self.aalloc([2, LP], BF16)
        yTT = [[T() for _ in TILES] for _ in range(2)]
        wo = self.aalloc([2, D], BF16)
        woT = T()
        wc = self.aalloc([8, 512], BF16)
        wcT = T()
        self.ld(wc, w_in[l, :, 0:512].rearrange("(c p) f -> p c f", p=128), w=[wcT], q="pool")
        cbuf = self.aalloc([2, 30 + LP], BF16)
        cbT = [T() for _ in TILES]
        cpadT = T()
        diag = self.aalloc([62, 128], BF16)
        diagT = T()
        yc = self.ring(2, [2, 512], F32)
        ysq = self.aalloc([2, 512], F32)
        ysqT = T()
        sg = self.ring(2, [512], F32)
        mu = self.aalloc([512], F32); muT = T()
        var = self.aalloc([512], F32); varT = T()
        dr = self.ring(2, [512], F32)
        self.S.op("dve", lambda e: e.memset(cbuf[:, :, 0:30], 0.0), writes=[cpadT])
        identb = self.CB[:, C_ID:C_ID + 128]
        for k in range(62):
            self.ts(diag[:, k, :], identb, self.pvc(l, P_CW + k), None, ALU.mult, r=[self.cT], w=[diagT])
        for ti, (t0, tw) in enumerate(TILES):
            for cc in range(2):
                pv_, pvT = self.nb()
                pg_, pgT = self.nb()
                for c in range(8):
                    self.mm(pv_[:, 0:tw], wc[:, c, cc * 128:(cc + 1) * 128], self.XN[:, c, t0:t0 + tw],
                            start=(c == 0), stop=(c == 7), r=[wcT, self.XNT[ti]], w=[pvT])
                for c in range(8):
                    self.mm(pg_[:, 0:tw], wc[:, c, 256 + cc * 128:256 + (cc + 1) * 128], self.XN[:, c, t0:t0 + tw],
                            start=(c == 0), stop=(c == 7), r=[wcT, self.XNT[ti]], w=[pgT])
                sa, sT = sg.next()
                self.act(sa[:, 0:tw], pg_[:, 0:tw], AF.Sigmoid, r=[pgT], w=[sT])
                self.tt(cbuf[:, cc, 30 + t0:30 + t0 + tw], pv_[:, 0:tw], sa[:, 0:tw], ALU.mult,
                        r=[pvT, sT], w=[cbT[ti]])
        onesF = self.CF[:, C_ONES:C_ONES + 128]
        for ti, (t0, tw) in enumerate(TILES):
            ya, yt = yc.next()
            deps = [diagT, cbT[ti], cpadT] + ([cbT[ti - 1]] if ti > 0 else [])
            for cc in range(2):
                po, poT = self.nb()
                for j in range(31):
                    self.mm(po[:, 0:tw], diag[:, cc * 31 + j, :], cbuf[:, cc, t0 + j:t0 + j + tw],
                            start=(j == 0), stop=(j == 30), r=deps, w=[poT])
                self.act(ya[:, cc, 0:tw], po[:, 0:tw], AF.Identity, bias=self.pvc(l, P_CB + cc),
                         r=[poT, self.cT], w=[yt])
                self.act(ysq[:, cc, 0:tw], ya[:, cc, 0:tw], AF.Square, r=[yt], w=[ysqT])
            s1, s1T = self.nb()
            s2, s2T = self.nb()
            for cc in range(2):
                self.mm(s1[:, 0:tw], onesF, ya[:, cc, 0:tw], start=(cc == 0), stop=(cc == 1), r=[yt, self.cT], w=[s1T])
            for cc in range(2):
                self.mm(s2[:, 0:tw], onesF, ysq[:, cc, 0:tw], start=(cc == 0), stop=(cc == 1), r=[ysqT, self.cT], w=[s2T])
            self.act(mu[:, 0:tw], s1[:, 0:tw], AF.Copy, scale=1.0 / 256, r=[s1T], w=[muT])
            self.act(var[:, 0:tw], s1[:, 0:tw], AF.Square, scale=1.0 / 256, r=[s1T], w=[varT])
            self.stt(var[:, 0:tw], s2[:, 0:tw], 1.0 / 256, var[:, 0:tw], ALU.mult, ALU.subtract, r=[s2T, varT], w=[varT])
            self.rstd_from(var[:, 0:tw], var[:, 0:tw], 1.0, self.EPS_LN, r=[varT], w=[varT])
            for cc in range(2):
                da, dT = dr.next()
                self.tt(da[:, 0:tw], ya[:, cc, 0:tw], mu[:, 0:tw], ALU.subtract, r=[yt, muT], w=[dT])
                self.stt(da[:, 0:tw], da[:, 0:tw], self.pvc(l, P_CLG + cc), var[:, 0:tw], ALU.mult, ALU.mult,
                         r=[dT, varT, self.cT], w=[dT])
                self.act(yT[:, cc, t0:t0 + tw], da[:, 0:tw], AF.Silu, bias=self.pvc(l, P_CLB + cc),
                         r=[dT, self.cT], w=[yTT[cc][ti]])
        self.wout(l, 0, 2, yT, yTT, wo, woT)


    def proj_shift(self, l, wsl, wT, mucol, ti, carry, carryT, pbuf, tmpd, tmpdT, dst, dstT):
        t0, tw = TILES[ti]
        pb_, pt_ = self.nb()
        for c in range(8):
            self.mm(pb_[:, 0:tw], wsl(c), self.XN[:, c, t0:t0 + tw], start=(c == 0), stop=(c == 7),
                    r=[wT, self.XNT[ti]], w=[pt_])
        pa, paT = pbuf.next()
        if ti == 0:
            self.S.op("dve", lambda e: e.memset(pa[:, 0:1], 0.0), writes=[paT])
        else:
            self.act(pa[:, 0:1], carry, AF.Copy, r=[carryT], w=[paT])
        self.act(pa[:, 1:1 + tw], pb_[:, 0:tw], AF.Copy, r=[pt_], w=[paT])
        self.act(carry, pa[:, tw:tw + 1], AF.Copy, r=[paT], w=[carryT])
        self.tt(tmpd[:, 0:tw], pa[:, 0:tw], pa[:, 1:1 + tw], ALU.subtract, r=[paT], w=[tmpdT])
        self.stt(dst, tmpd[:, 0:tw], self.pvc(l, mucol), pa[:, 1:1 + tw], ALU.mult, ALU.add,
                 r=[tmpdT, paT, self.cT], w=[dstT])

    def rwkv_branch(self, l):
        self.aoff = 0
        self._nbn = 6
        w_in = self.d["w_in"]
        yT = self.aalloc([1, LP], BF16)
        wo = self.aalloc([1, D], BF16); woT = T()
        wl = self.aalloc([8, 256], BF16); wlT = T()
        LB = self.aalloc([384], BF16); LBT = T()
        GBw = self.aalloc([384], BF16); GBT = T()
        self.ld(wl, w_in[l, :, 1664:1920].rearrange("(c p) f -> p c f", p=128), w=[wlT], q="pool")
        self.ld(LB[0:64, :], self.d["rwkv_wB"][l], w=[LBT], q="pool")
        self.ld(LB[64:128, :], self.d["rwkv_aB"][l], w=[LBT], q="pool")
        self.ld(GBw, self.d["rwkv_gB"][l], w=[GBT], q="pool")
        lora1 = self.aalloc([LP], BF16)
        sgd = self.aalloc([LP], BF16)
        loraT = [T() for _ in TILES]
        pbuf = self.ring(1, [513], F32)
        carry = self.aalloc([12], F32)
        carryT = [T() for _ in range(12)]
        F = [self.aalloc([512], F32) for _ in range(9)]
        FT = [T() for _ in range(9)]
        tmpd, tmpdT = F[8], FT[8]
        csbuf = self.aalloc([576], F32); csT = T()
        xvb = self.aalloc([512], BF16); xvbT = T()
        kksq = self.aalloc([512], BF16); kksqT = T()
        BT = self.aalloc([512], BF16); BTT = T()
        KT = self.aalloc([512], BF16); KTT = T()
        NR = self.aalloc([8, 128], BF16); NRT = T()
        tm = self.aalloc([4, 8, 64], BF16); tmT = T()
        M1s = self.aalloc([8, 128], BF16); M1T = T()
        M2s = self.aalloc([8, 128], BF16); M2T = T()
        Pr = self.ring(2, [8, 64], BF16)
        Qr = self.ring(2, [8, 64], BF16)
        TTr = self.ring(2, [8, 64], BF16)
        Xs = self.aalloc([8, 64], BF16); XsT = T()
        Wtm, WtmT = Xs, XsT
        Ub2 = self.aalloc([8, 64], BF16); Ub2T = T()
        GTb = [self.aalloc([8, 64], BF16) for _ in range(2)]; GTbT = [T(), T()]
        RWb = [self.aalloc([8, 64], BF16) for _ in range(2)]; RWbT = [T(), T()]
        Cpb = [self.aalloc([8, 64], F32) for _ in range(2)]; CpbT = [T(), T()]
        Ddb = [self.aalloc([8, 64], BF16) for _ in range(2)]; DdbT = [T(), T()]
        GCb = [self.aalloc([8], F32) for _ in range(2)]; GCbT = [T(), T()]
        gbufb = [self.aalloc([512], BF16) for _ in range(2)]; gbT = [T(), T()]
        bonb = [self.aalloc([512], BF16) for _ in range(2)]; bonT = [T(), T()]
        Sbr = self.ring(2, [64], BF16)
        FO = [self.aalloc([512], F32) for _ in range(3)]
        FOT = [T() for _ in range(3)]
        wr = self.aalloc([8, 3, 128], BF16); wrT = T()
        self.S.op("dve", lambda e: e.memset(csbuf[:, 0:64], 0.0), writes=[csT])
        for ti, (t0, tw) in enumerate(TILES):
            self.proj_shift(l, lambda c: wl[:, c, 0:128], wlT, P_MU + 9, ti, carry[:, 9:10], carryT[9],
                            pbuf, tmpd, tmpdT, F[0][:, 0:tw], FT[0])
            self.act(lora1[0:64, t0:t0 + tw], F[0][0:64, 0:tw], AF.Tanh, r=[FT[0]], w=[loraT[ti]])
            self.cp(lora1[64:128, t0:t0 + tw], F[0][64:128, 0:tw], r=[FT[0]], w=[loraT[ti]])
            self.proj_shift(l, lambda c: wl[:, c, 128:256], wlT, P_MU + 10, ti, carry[:, 10:11], carryT[10],
                            pbuf, tmpd, tmpdT, F[1][:, 0:tw], FT[1])
            self.act(sgd[:, t0:t0 + tw], F[1][:, 0:tw], AF.Sigmoid, r=[FT[1]], w=[loraT[ti]])
        self.arena_peak = max(getattr(self, 'arena_peak', 0), self.aoff)
        bones = self.CB[:, C_BONES:C_BONES + 128]
        bonesF = self.CF[:, C_BONES:C_BONES + 128]
        maskM = self.CB[:, C_CMU:C_CMU + 128]
        maskL = self.CB[:, C_CML:C_CML + 64]
        i64 = self.CB[:, C_I64:C_I64 + 64]
        R2 = [slice(0, 64), slice(64, 128)]
        Ob = [(self.pb[6], self.pbT[6]), (self.pb[7], self.pbT[7])]

        def both(emit):
            emit(0, R2[0])
            self.S.fence("pe")
            emit(1, R2[1])
            self.S.fence("pe")
        xr, xk, a_, sgw, t1, inv, ci, Ei, Ee = F
        xrT, xkT, aT, sgwT, t1T, invT, ciT, EiT, EeT = FT
        yTT = [[T() for _ in TILES]]
        v3 = lambda ap, n, b=64: ap.rearrange("p (a b) -> p a b", b=b)[:, 0:n, :]

        def prep(j, ti):
            t0, tw = TILES[ti]
            nck = tw // 64
            s = ti % 2
            v2 = lambda ap: ap[:, 0:tw].rearrange("p (a b) -> p a b", b=64)
            self.proj_shift(l, lambda c: wr[:, c, 0, :], wrT, P_MU + j, ti, carry[:, j:j + 1], carryT[j],
                            pbuf, tmpd, tmpdT, xr[:, 0:tw], xrT)
            yield
            self.proj_shift(l, lambda c: wr[:, c, 1, :], wrT, P_MU + 3 + j, ti, carry[:, 3 + j:4 + j], carryT[3 + j],
                            pbuf, tmpd, tmpdT, xk[:, 0:tw], xkT)
            yield
            self.proj_shift(l, lambda c: wr[:, c, 2, :], wrT, P_MU + 6 + j, ti, carry[:, 6 + j:7 + j], carryT[6 + j],
                            pbuf, tmpd, tmpdT, xvb[:, 0:tw], xvbT)
            yield
            pw, pwT = self.nb()
            self.mm(pw[:, 0:tw], LB[0:64, j * 128:(j + 1) * 128], lora1[0:64, t0:t0 + tw], r=[LBT, loraT[ti]], w=[pwT])
            self.act(sgw[:, 0:tw], pw[:, 0:tw], AF.Sigmoid, bias=self.pvc(l, P_W0 + j), r=[pwT, self.cT], w=[sgwT])
            pa2, pa2T = self.nb()
            self.mm(pa2[:, 0:tw], LB[64:128, j * 128:(j + 1) * 128], lora1[64:128, t0:t0 + tw],
                    r=[LBT, loraT[ti]], w=[pa2T])
            self.act(a_[:, 0:tw], pa2[:, 0:tw], AF.Sigmoid, bias=self.pvc(l, P_A0 + j), r=[pa2T, self.cT], w=[aT])
            pg2, pg2T = self.nb()
            self.mm(pg2[:, 0:tw], GBw[:, j * 128:(j + 1) * 128], sgd[:, t0:t0 + tw], r=[GBT, loraT[ti]], w=[pg2T])
            self.act(gbufb[s][:, 0:tw], pg2[:, 0:tw], AF.Copy, r=[pg2T], w=[gbT[s]])
            yield
            self.act(kksq[:, 0:tw], xk[:, 0:tw], AF.Square, scale=self.pvc(l, P_KK + j), r=[xkT, self.cT], w=[kksqT])
            pss, pssT = self.nb()
            self.mm(pss[:, 0:tw], bones, kksq[:, 0:tw], r=[kksqT, self.cT], w=[pssT])
            self.ts(inv[:, 0:tw], pss[:, 0:tw], 1e-24, None, ALU.max, r=[pssT], w=[invT])
            self.act(inv[:, 0:tw], inv[:, 0:tw], AF.Ln, r=[invT], w=[invT])
            self.act(inv[:, 0:tw], inv[:, 0:tw], AF.Exp, scale=-0.5, r=[invT], w=[invT])
            self.ts(t1[:, 0:tw], a_[:, 0:tw], self.pvc(l, P_KA + j), self.PD[:, l * 8 + j:l * 8 + j + 1],
                    ALU.mult, ALU.add, r=[aT, self.cT], w=[t1T])
            self.tt(t1[:, 0:tw], xk[:, 0:tw], t1[:, 0:tw], ALU.mult, r=[xkT, t1T], w=[t1T])
            self.stt(xk[:, 0:tw], xk[:, 0:tw], self.pvc(l, P_KK + j), inv[:, 0:tw], ALU.mult, ALU.mult,
                     r=[xkT, invT, self.cT], w=[xkT])
            yield
            self.S.op("dve", lambda e: e.tensor_tensor_scan(out=csbuf[:, 64:64 + tw], data0=sgw[:, 0:tw],
                                                            data1=sgw[:, 0:tw], initial=0.0,
                                                            op0=ALU.add, op1=ALU.bypass),
                      reads=[sgwT], writes=[csT])
            csv = csbuf[:, 64:64 + tw].rearrange("p (a b) -> p a b", b=64)
            base = csbuf[:, 0:tw].rearrange("p (a b) -> p a b", b=64)[:, :, 63:64].to_broadcast([128, nck, 64])
            self.tt(v2(ci), csv, base, ALU.subtract, r=[csT], w=[ciT])
            self.tt(Ee[:, 0:tw], ci[:, 0:tw], sgw[:, 0:tw], ALU.subtract, r=[ciT, sgwT], w=[EeT])
            self.act(Ee[:, 0:tw], Ee[:, 0:tw], AF.Exp, scale=-DECAY_C, r=[EeT], w=[EeT])
            self.act(Ei[:, 0:tw], ci[:, 0:tw], AF.Exp, scale=-DECAY_C, r=[ciT], w=[EiT])
            self.act(ci[:, 0:tw], ci[:, 0:tw], AF.Exp, scale=DECAY_C, r=[ciT], w=[ciT])
            yield
            En = ci
            self.stt(NR[:, 0:nck, 0:64], v2(xk), -1.0, v2(Ee), ALU.mult, ALU.mult, r=[xkT, EeT], w=[NRT])
            self.tt(NR[:, 0:nck, 64:128], v2(xr), v2(Ei), ALU.mult, r=[xrT, EiT], w=[NRT])
            self.tt(inv[:, 0:tw], xk[:, 0:tw], a_[:, 0:tw], ALU.mult, r=[xkT, aT], w=[invT])
            self.tt(BT[:, 0:tw], inv[:, 0:tw], En[:, 0:tw], ALU.mult, r=[invT, ciT], w=[BTT])
            self.tt(KT[:, 0:tw], t1[:, 0:tw], En[:, 0:tw], ALU.mult, r=[t1T, ciT], w=[KTT])
            self.act(GCb[s][:, 0:nck], v2(Ei)[:, :, 63], AF.Copy, r=[EiT], w=[GCbT[s]])
            self.stt(kksq[:, 0:tw], xr[:, 0:tw], self.pvc(l, P_RK + j), t1[:, 0:tw], ALU.mult, ALU.mult,
                     r=[xrT, t1T, self.cT], w=[kksqT])
            pbn, pbnT = self.nb()
            self.mm(pbn[:, 0:tw], bones, kksq[:, 0:tw], r=[kksqT, self.cT], w=[pbnT])
            self.tt(bonb[s][:, 0:tw], pbn[:, 0:tw], xvb[:, 0:tw], ALU.mult, r=[pbnT, xvbT], w=[bonT[s]])
            yield
            srcs = [(lambda c: NR[:, c, 0:64], NRT), (lambda c: BT[:, c * 64:(c + 1) * 64], BTT),
                    (lambda c: KT[:, c * 64:(c + 1) * 64], KTT), (lambda c: xvb[:, c * 64:(c + 1) * 64], xvbT)]
            for half in range(2):
                bk, bkT = self.nb()
                bkb = bk[:, :].bitcast(BF16)
                def em(e, rows, half=half, bkb=bkb, bkT=bkT):
                    for q in range(2):
                        fn, st = srcs[half * 2 + q]
                        for c in range(nck):
                            self.tr(bkb[rows, q * 512 + c * 64:q * 512 + (c + 1) * 64], fn(c)[rows, :],
                                    self.CB[rows, C_I64:C_I64 + 64], r=[st, self.cT], w=[bkT])
                both(em)
                self.cp(tm[:, half * 2:half * 2 + 2, 0:nck, :],
                        bkb.rearrange("p (q c k) -> p q c k", q=2, k=64)[:, :, 0:nck, :], r=[bkT], w=[tmT])
            yield
            nh = (nck + 3) // 4
            bM1 = [self.nb() for _ in range(nh)]
            bM2 = [self.nb() for _ in range(nh)]
            bA, bAT = self.nb()
            def em(e, rows):
                for c in range(nck):
                    cs_ = slice((c % 4) * 128, (c % 4) * 128 + 128)
                    self.mm(bM1[c // 4][0][rows, cs_], BT[rows, c * 64:(c + 1) * 64], NR[rows, c, :],
                            r=[BTT, NRT], w=[bM1[c // 4][1]])
                    self.mm(bM2[c // 4][0][rows, cs_], KT[rows, c * 64:(c + 1) * 64], NR[rows, c, :],
                            r=[KTT, NRT], w=[bM2[c // 4][1]])
                    self.mm(bA[rows, c * 64:(c + 1) * 64], NR[rows, c, 0:64], BT[rows, c * 64:(c + 1) * 64],
                            r=[BTT, NRT], w=[bAT])
            both(em)
            for hh in range(nh):
                n4 = min(4, nck - hh * 4)
                mb = maskM.unsqueeze(1).to_broadcast([128, n4, 128])
                self.tt(M1s[:, hh * 4:hh * 4 + n4, :], v3(bM1[hh][0][:, 0:n4 * 128], n4, 128),
                        mb, ALU.mult, r=[bM1[hh][1], self.cT], w=[M1T])
                self.tt(M2s[:, hh * 4:hh * 4 + n4, :], v3(bM2[hh][0][:, 0:n4 * 128], n4, 128),
                        mb, ALU.mult, r=[bM2[hh][1], self.cT], w=[M2T])
            P, PT_ = Pr.next()
            self.tt(P[:, 0:nck, :], v3(bA[:, 0:tw], nck), maskL.unsqueeze(1).to_broadcast([128, nck, 64]), ALU.mult,
                    r=[bAT, self.cT], w=[PT_])
            Q, QT_ = Qr.next()
            self.act(Q[:, 0:nck, :], M1s[:, 0:nck, 0:64], AF.Copy, r=[M1T], w=[QT_])
            TTa, TTT = TTr.next()
            self.tt(TTa[:, 0:nck, :], M1s[:, 0:nck, 0:64], i64.unsqueeze(1).to_broadcast([128, nck, 64]), ALU.add,
                    r=[M1T, self.cT], w=[TTT])
            yield
            for rnd in range(1, 7):
                doP = rnd <= 5
                doQ = rnd <= 4
                doT = rnd >= 2
                if doP:
                    bP, bPT = self.nb()
                if doQ:
                    bQ, bQT = self.nb()
                if doT:
                    bT, bTT = self.nb()
                if rnd == 1:
                    bX, bXT = self.nb()
                def em(e, rows):
                    for c in range(nck):
                        cs_ = slice(c * 64, (c + 1) * 64)
                        if rnd == 1:
                            self.mm(bX[rows, cs_], M2s[rows, c, 0:64], tm[rows, 3, c, :], r=[tmT, M2T], w=[bXT])
                        if doP:
                            self.mm(bP[rows, cs_], Q[rows, c, :], P[rows, c, :], r=[QT_, PT_], w=[bPT])
                        if doQ:
                            self.mm(bQ[rows, cs_], P[rows, c, :], Q[rows, c, :], r=[QT_, PT_], w=[bQT])
                        if doT:
                            self.mm(bT[rows, cs_], P[rows, c, :], TTa[rows, c, :], r=[PT_, TTT], w=[bTT])
                both(em)
                if doT:
                    TT2, TT2T = TTr.next()
                    self.tt(TT2[:, 0:nck, :], v3(bT[:, 0:tw], nck), TTa[:, 0:nck, :], ALU.add, r=[bTT, TTT], w=[TT2T])
                    TTa, TTT = TT2, TT2T
                if doP:
                    P2, P2T = Pr.next()
                    self.cp(P2[:, 0:nck, :], v3(bP[:, 0:tw], nck), r=[bPT], w=[P2T])
                if doQ:
                    Q2, Q2T = Qr.next()
                    self.act(Q2[:, 0:nck, :], v3(bQ[:, 0:tw], nck), AF.Copy, r=[bQT], w=[Q2T])
                    Q, QT_ = Q2, Q2T
                if doP:
                    P, PT_ = P2, P2T
                if rnd == 1:
                    self.act(Xs[:, 0:nck, :], v3(bX[:, 0:tw], nck), AF.Copy, r=[bXT], w=[XsT])
                yield
            bW, bWT = self.nb()
            bU, bUT = self.nb()
            def em(e, rows):
                for c in range(nck):
                    cs_ = slice(c * 64, (c + 1) * 64)
                    self.mm(bW[rows, cs_], TTa[rows, c, :], tm[rows, 0, c, :], r=[tmT, TTT], w=[bWT])
                    self.mm(bU[rows, cs_], TTa[rows, c, :], Xs[rows, c, :], r=[TTT, XsT], w=[bUT])
            both(em)
            self.cp(Wtm[:, 0:nck, :], v3(bW[:, 0:tw], nck), r=[bWT], w=[WtmT])
            Utb, UtbT = Ub2, Ub2T
            self.act(Utb[:, 0:nck, :], v3(bU[:, 0:tw], nck), AF.Copy, r=[bUT], w=[UtbT])
            yield
            bG, bGT = self.nb()
            bR, bRT = self.nb()
            bC, bCT = self.nb()
            bD, bDT = self.nb()
            def em(e, rows):
                for c in range(nck):
                    cs_ = slice(c * 64, (c + 1) * 64)
                    self.mm(bG[rows, cs_], Wtm[rows, c, :], tm[rows, 1, c, :], r=[WtmT, tmT], w=[bGT])
                    self.mm(bR[rows, cs_], Wtm[rows, c, :], M1s[rows, c, 64:128], r=[WtmT, M1T], w=[bRT])
                    self.mm(bC[rows, cs_], tm[rows, 1, c, :], Utb[rows, c, :], start=True, stop=False, r=[tmT, UtbT], w=[bCT])
                    self.mm(bC[rows, cs_], tm[rows, 2, c, :], tm[rows, 3, c, :], start=False, stop=True, r=[tmT], w=[bCT])
                    self.mm(bD[rows, cs_], Utb[rows, c, :], M1s[rows, c, 64:128], start=True, stop=False, r=[UtbT, M1T], w=[bDT])
                    self.mm(bD[rows, cs_], tm[rows, 3, c, :], M2s[rows, c, 64:128], start=False, stop=True, r=[tmT, M2T], w=[bDT])
            both(em)
            self.tt(GTb[s][:, 0:nck, :], v3(bG[:, 0:tw], nck), i64.unsqueeze(1).to_broadcast([128, nck, 64]), ALU.add,
                    r=[bGT, self.cT], w=[GTbT[s]])
            self.tt(RWb[s][:, 0:nck, :], v3(bR[:, 0:tw], nck), NR[:, 0:nck, 64:128], ALU.add, r=[bRT, NRT], w=[RWbT[s]])
            self.tt(Cpb[s][:, 0:nck, :], v3(bC[:, 0:tw], nck), GCb[s][:, 0:nck].unsqueeze(2).to_broadcast([128, nck, 64]),
                    ALU.mult, r=[bCT, GCbT[s]], w=[CpbT[s]])
            self.act(Ddb[s][:, 0:nck, :], v3(bD[:, 0:tw], nck), AF.Copy, r=[bDT], w=[DdbT[s]])
            yield

        state = {}

        def seq(j, ti):
            t0, tw = TILES[ti]
            nck = tw // 64
            s = ti % 2
            Sb, SbT = state["S"]
            for c in range(nck):
                bSe = [self.nb(), self.nb()]
                Sn, SnT = Sbr.next()
                for e in range(2):
                    rows = R2[e]
                    self.mm(bSe[e][0][rows, 0:64], GTb[s][rows, c, :], Sb[rows, :], r=[GTbT[s], SbT], w=[bSe[e][1]])
                    self.mm(Ob[e][0][rows, c * 64:(c + 1) * 64], Sb[rows, :], RWb[s][rows, c, :], r=[SbT, RWbT[s]],
                            w=[Ob[e][1]])
                for e in range(2):
                    rows = R2[e]
                    self.stt(Sn[rows, :], bSe[e][0][rows, 0:64], GCb[s][rows, c:c + 1], Cpb[s][rows, c, :],
                             ALU.mult, ALU.add, r=[bSe[e][1], GCbT[s], CpbT[s]], w=[SnT])
                Sb, SbT = Sn, SnT
                yield
            state["S"] = (Sb, SbT)
            Osb, Osq, var = FO
            OsbT, OsqT, varT = FOT
            for e in range(2):
                rows = R2[e]
                self.tt(Osb[rows, 0:tw], Ob[e][0][rows, 0:tw], Ddb[s][rows, 0:nck, :].rearrange("p a b -> p (a b)"),
                        ALU.add, r=[Ob[e][1], DdbT[s]], w=[OsbT])
            self.act(Osq[:, 0:tw], Osb[:, 0:tw], AF.Square, r=[OsbT], w=[OsqT])
            s1, s1T = self.nb()
            s2, s2T = self.nb()
            self.mm(s1[:, 0:tw], bonesF, Osb[:, 0:tw], r=[OsbT, self.cT], w=[s1T])
            self.mm(s2[:, 0:tw], bonesF, Osq[:, 0:tw], r=[OsqT, self.cT], w=[s2T])
            self.act(var[:, 0:tw], s1[:, 0:tw], AF.Square, scale=1.0 / 64, r=[s1T], w=[varT])
            self.stt(var[:, 0:tw], s2[:, 0:tw], 1.0 / 64, var[:, 0:tw], ALU.mult, ALU.subtract, r=[s2T, varT], w=[varT])
            self.rstd_from(var[:, 0:tw], var[:, 0:tw], 1.0, self.EPS_GN, r=[varT], w=[varT])
            dd, ddT = Osq, OsqT
            self.stt(dd[:, 0:tw], s1[:, 0:tw], -1.0 / 64, Osb[:, 0:tw], ALU.mult, ALU.add, r=[s1T, OsbT], w=[ddT])
            self.tt(dd[:, 0:tw], dd[:, 0:tw], var[:, 0:tw], ALU.mult, r=[ddT, varT], w=[ddT])
            self.stt(dd[:, 0:tw], dd[:, 0:tw], self.pvc(l, P_LG + j), bonb[s][:, 0:tw], ALU.mult, ALU.add,
                     r=[ddT, bonT[s], self.cT], w=[ddT])
            self.stt(yT[:, 0, t0:t0 + tw], dd[:, 0:tw], self.pvc(l, P_LB + j), gbufb[s][:, 0:tw], ALU.add, ALU.mult,
                     r=[ddT, gbT[s], self.cT], w=[yTT[0][ti]])
            yield

        def drain(g):
            for _ in g:
                pass

        for j in range(3):
            for q, c0 in enumerate((512, 896, 1280)):
                self.ld(wr[:, :, q, :], w_in[l, :, c0 + j * 128:c0 + (j + 1) * 128].rearrange("(c p) f -> p c f", p=128),
                        w=[wrT], q="pool")
            S0, S0T = Sbr.next()
            self.S.op("dve", lambda e, S0=S0: e.memset(S0[:, :], 0.0), writes=[S0T])
            state["S"] = (S0, S0T)
            drain(prep(j, 0))
            for ti in range(len(TILES)):
                sg_ = seq(j, ti)
                pg_ = prep(j, ti + 1) if ti + 1 < len(TILES) else None
                nsteps = TILES[ti][1] // 64 + 1
                per = -(-18 // nsteps)
                alive_s, alive_p = True, pg_ is not None
                while alive_s or alive_p:
                    if alive_s:
                        try:
                            next(sg_)
                        except StopIteration:
                            alive_s = False
                    if alive_p:
                        for _ in range(per):
                            try:
                                next(pg_)
                            except StopIteration:
                                alive_p = False
                                break
            self.wout(l, 256 + j * 128, 1, yT, yTT, wo, woT)

    def sb_branch(self, l):
        self.aoff = 0
        w_in = self.d["w_in"]
        yT = self.aalloc([1, LP], BF16)
        wo = self.aalloc([1, D], BF16)
        woT = T()
        wq = self.aalloc([8, 384], BF16); wqT = T()
        wk = self.aalloc([8, 384], BF16); wkT = T()
        wv = self.aalloc([8, 384], BF16); wvT = T()
        for wa, wt, c0 in ((wq, wqT, 1920), (wk, wkT, 2304), (wv, wvT, 2688)):
            self.ld(wa, w_in[l, :, c0:c0 + 384].rearrange("(c p) f -> p c f", p=128), w=[wt], q="pool")
        qT = self.aalloc([3, LP], BF16)
        kT = self.aalloc([3, LP], BF16)
        vtm = self.aalloc([17, 384], BF16)
        qTT = [[T() for _ in TILES] for _ in range(3)]
        kTT = [[T() for _ in range(17)] for _ in range(3)]
        vT = [[T() for _ in range(17)] for _ in range(3)]
        Er = self.ring(2, [512], F32)
        spr = self.ring(3, [512], BF16)
        Ar = self.ring(3, [512], BF16)
        Rr = self.ring(3, [512], BF16)
        qz = [[self.aalloc([512], BF16) for _ in range(2)] for _ in range(2)]
        qzT = [[T() for _ in range(2)] for _ in range(2)]
        Osb = self.aalloc([512], F32); OsbT = T()
        sqb = self.aalloc([512], F32); sqT = T()
        rs = self.aalloc([512], F32); rsT = T()
        for s_ in range(2):
            self.S.op("dve", lambda e, s_=s_: e.memset(qz[s_][0][64:128, :], 0.0), writes=[qzT[s_][0]])
            self.S.op("dve", lambda e, s_=s_: e.memset(qz[s_][1][0:64, :], 0.0), writes=[qzT[s_][1]])
        pk_ = [0]

        def pbank():
            b_ = 6 + pk_[0] % 2
            pk_[0] += 1
            return self.pb[b_], self.pbT[b_]

        def proj_units(j):
            for ti, (t0, tw) in enumerate(TILES):
                pq, pqT = pbank()
                for c in range(8):
                    self.mm(pq[:, 0:tw], wq[:, c, j * 128:(j + 1) * 128], self.XN[:, c, t0:t0 + tw],
                            start=(c == 0), stop=(c == 7), r=[wqT, self.XNT[ti]], w=[pqT])
                self.ts(qT[:, j, t0:t0 + tw], pq[:, 0:tw], 0.125, None, ALU.mult, r=[pqT], w=[qTT[j][ti]])
                yield
                pk, pkT = pbank()
                for c in range(8):
                    self.mm(pk[:, 0:tw], wk[:, c, j * 128:(j + 1) * 128], self.XN[:, c, t0:t0 + tw],
                            start=(c == 0), stop=(c == 7), r=[wkT, self.XNT[ti]], w=[pkT])
                kts = [kTT[j][b] for b in range(t0 // 128, (t0 + tw) // 128)]
                self.cp(kT[:, j, t0:t0 + tw], pk[:, 0:tw], r=[pkT], w=kts)
                yield
            for blk in range(17):
                ti = min(blk // 4, 4)
                pv_, pvT = pbank()
                for c in range(8):
                    self.mm(pv_[:, 0:128], self.XN[:, c, blk * 128:(blk + 1) * 128], wv[:, c, j * 128:(j + 1) * 128],
                            start=(c == 0), stop=(c == 7), r=[wvT, self.XNT[ti]], w=[pvT])
                self.cp(vtm[:, blk, j * 128:(j + 1) * 128], pv_[:, 0:128], r=[pvT], w=[vT[j][blk]])
                yield

        for _ in proj_units(0):
            pass
        ntri = self.CB[:, C_NTRI:C_NTRI + 128]
        nones = self.CB[:, C_NONES:C_NONES + 128]
        bonesF = self.CF[:, C_BONES:C_BONES + 128]
        one = self.PD[:, self.ONE:self.ONE + 1]
        blocks = []
        g = 0
        for j in range(3):
            for ti, (t0, tw) in enumerate(TILES):
                nkb = (t0 + tw) // 128
                for e in range(2):
                    for kb in reversed(range(nkb)):
                        blocks.append(dict(j=j, ti=ti, e=e, kb=kb, nkb=nkb, g=g))
                g += 1
        zi = [0]
        chain = {}
        yTT1 = [[T() for _ in TILES]]
        yTTs = {j: yTT1 for j in range(3)}

        def s0(b):
            j, ti, e, kb, g_ = b["j"], b["ti"], b["e"], b["kb"], b["g"]
            t0, tw = TILES[ti]
            st = g_ % 2
            if e == 0 and kb == b["nkb"] - 1:
                self.cp(qz[st][0][0:64, 0:tw], qT[0:64, j, t0:t0 + tw], r=[qTT[j][ti]], w=[qzT[st][0]])
                self.cp(qz[st][1][64:128, 0:tw], qT[64:128, j, t0:t0 + tw], r=[qTT[j][ti]], w=[qzT[st][1]])
            Z, ZT = self.pb[zi[0] % 3], self.pbT[zi[0] % 3]
            zi[0] += 1
            b["Z"], b["ZT"] = Z, ZT
            self.mm(Z[:, 0:tw], kT[:, j, kb * 128:(kb + 1) * 128], qz[st][e][:, 0:tw], start=True, stop=False,
                    r=[kTT[j][kb], qzT[st][e]], w=[ZT], sgc=True)
            Ea, ET = Er.next()
            self.act(Ea[:, 0:tw], Z[:, 0:tw], AF.Exp, r=[ZT], w=[ET])
            sa, sT = spr.next()
            b["sp"], b["spT"] = sa, sT
            self.act(sa[:, 0:tw], Ea[:, 0:tw], AF.Ln, bias=one, r=[ET, self.cT], w=[sT])
            off = kb * 128 - t0
            b["mask"] = None
            if off >= 0:
                mask = self.CB[:, C_SBM + (off // 128) * 512:C_SBM + (off // 128) * 512 + tw]
                b["mask"] = mask
                self.tt(sa[:, 0:tw], sa[:, 0:tw], mask, ALU.mult, r=[sT, self.cT], w=[sT])

        def s1(b):
            j, ti, e, kb = b["j"], b["ti"], b["e"], b["kb"]
            t0, tw = TILES[ti]
            first = kb == b["nkb"] - 1
            Z, ZT, sa, sT = b["Z"], b["ZT"], b["sp"], b["spT"]
            key = (j, ti, e)
            self.mm(Z[:, 0:tw], ntri, sa[:, 0:tw], start=False, stop=first, r=[sT, self.cT], w=[ZT], sgc=True)
            if not first:
                Rb, RbT = chain[key]
                self.mm(Z[:, 0:tw], nones, Rb[:, 0:tw], start=False, stop=True, r=[RbT, self.cT], w=[ZT], sgc=True)
            Aa, AT_ = Ar.next()
            b["A"], b["AT"] = Aa, AT_
            self.act(Aa[:, 0:tw], Z[:, 0:tw], AF.Exp, r=[ZT], w=[AT_])
            if b["mask"] is not None:
                self.tt(Aa[:, 0:tw], Aa[:, 0:tw], b["mask"], ALU.mult, r=[AT_, self.cT], w=[AT_])
            if kb > 0:
                Rn, RnT = Rr.next()
                if first:
                    self.cp(Rn[:, 0:tw], sa[:, 0:tw], r=[sT], w=[RnT])
                else:
                    self.tt(Rn[:, 0:tw], Rb[:, 0:tw], sa[:, 0:tw], ALU.add, r=[RbT, sT], w=[RnT])
                chain[key] = (Rn, RnT)

        def s2(b):
            j, ti, e, kb, g_ = b["j"], b["ti"], b["e"], b["kb"], b["g"]
            t0, tw = TILES[ti]
            first = kb == b["nkb"] - 1
            ob = 3
            O, OT = self.pb[ob + e], self.pbT[ob + e]
            self.mm(O[:, 0:tw], vtm[:, kb, j * 128:(j + 1) * 128], b["A"][:, 0:tw], start=first, stop=(kb == 0),
                    r=[vT[j][kb], b["AT"]], w=[OT])
            if e == 1 and kb == 0:
                for ee in range(2):
                    rows = slice(ee * 64, ee * 64 + 64)
                    self.cp(Osb[rows, 0:tw], self.pb[ob + ee][rows, 0:tw], r=[self.pbT[ob + ee]], w=[OsbT])
                self.tt(sqb[:, 0:tw], Osb[:, 0:tw], Osb[:, 0:tw], ALU.mult, r=[OsbT], w=[sqT])
                ss, ssT = self.pb[5], self.pbT[5]
                self.mm(ss[:, 0:tw], bonesF, sqb[:, 0:tw], r=[sqT, self.cT], w=[ssT])
                self.rstd_from(rs[:, 0:tw], ss[:, 0:tw], 1.0 / 64, self.EPS_RMS, r=[ssT], w=[rsT])
                self.stt(yT[:, 0, t0:t0 + tw], Osb[:, 0:tw], self.pvc(l, P_SBN + j), rs[:, 0:tw], ALU.mult, ALU.mult,
                         r=[OsbT, rsT, self.cT], w=[yTTs[j][0][ti]])
                if ti == len(TILES) - 1:
                    self.wout(l, 640 + j * 128, 1, yT, yTTs[j], wo, woT, banks=[6, 7])

        n = len(blocks)
        pgen = {1: proj_units(1), 2: proj_units(2)}
        for i in range(n + 2):
            if i < n:
                jn = blocks[i]["j"] + 1
                if jn in pgen and i % 6 == 0:
                    next(pgen[jn], None)
                if i > 0 and blocks[i]["j"] != blocks[i - 1]["j"] and blocks[i]["j"] in pgen:
                    for _ in pgen[blocks[i]["j"]]:
                        pass
                s0(blocks[i])
            if 0 <= i - 1 < n:
                s1(blocks[i - 1])
            if 0 <= i - 2 < n:
                s2(blocks[i - 2])


def pack_params(inp, nl):
    def col(v):
        v = np.asarray(v, np.float32).reshape(-1)
        return v.reshape(-1, 128).T

    pv = np.zeros((128, nl * NPV + 8), np.float32)
    for l in range(nl):
        o = l * NPV
        pv[:, o + P_F1N:o + P_F1N + 8] = col(inp["ffn1_norm"][l])
        pv[:, o + P_MXN:o + P_MXN + 8] = col(inp["mix_norm"][l])
        pv[:, o + P_F2N:o + P_F2N + 8] = col(inp["ffn2_norm"][l])
        cw = np.asarray(inp["conv_w"][l], np.float32)
        for cc in range(2):
            pv[:, o + P_CW + cc * 31:o + P_CW + (cc + 1) * 31] = cw[:, cc * 128:(cc + 1) * 128].T
        pv[:, o + P_CB:o + P_CB + 2] = col(inp["conv_b"][l])
        pv[:, o + P_CLG:o + P_CLG + 2] = col(inp["conv_ln_g"][l])
        pv[:, o + P_CLB:o + P_CLB + 2] = col(inp["conv_ln_b"][l])
        pv[:, o + P_MU:o + P_MU + 11] = col(inp["rwkv_mu"][l])
        pv[:, o + P_W0:o + P_W0 + 3] = col(inp["rwkv_w0"][l])
        pv[:, o + P_A0:o + P_A0 + 3] = col(inp["rwkv_a0"][l])
        pv[:, o + P_KK:o + P_KK + 3] = col(inp["rwkv_kk"][l])
        pv[:, o + P_KA:o + P_KA + 3] = col(inp["rwkv_ka"][l])
        pv[:, o + P_RK:o + P_RK + 3] = col(inp["rwkv_rk"][l])
        pv[:, o + P_LG:o + P_LG + 3] = col(inp["rwkv_ln_g"][l])
        pv[:, o + P_LB:o + P_LB + 3] = col(inp["rwkv_ln_b"][l])
        pv[:, o + P_SBN:o + P_SBN + 3] = col(inp["sb_norm"][l])
    pv[:, nl * NPV:nl * NPV + 8] = col(inp["final_norm"])
    return pv


def make_consts():
    c = np.zeros((128, NCST), np.float32)
    i = np.arange(128)[:, None]
    j = np.arange(128)[None, :]
    c[:, C_ID:C_ID + 128] = (i == j)
    c[:, C_ONES:C_ONES + 128] = 1.0
    c[:, C_BONES:C_BONES + 128] = ((i // 64) == (j // 64))
    c[:, C_NTRI:C_NTRI + 128] = -(i >= j).astype(np.float32)
    c[:, C_NONES:C_NONES + 128] = -1.0
    u = np.arange(512)[None, :]
    for jj in range(4):
        c[:, C_SBM + jj * 512:C_SBM + (jj + 1) * 512] = (u > 128 * jj + i)
    s = (np.arange(128) % 64)[:, None]
    t = np.arange(64)[None, :]
    c[:, C_CMU:C_CMU + 64] = (s < t)
    c[:, C_CMU + 64:C_CMU + 128] = (s <= t)
    c[:, C_CML:C_CML + 64] = (s > t)
    c[:, C_I64:C_I64 + 64] = (s == t)
    return c


_CACHE = {}


def run(inputs, nl=NL_FULL, do_mixer=True, dbg=None, branches=("conv", "rwkv", "sb")):
    key = (nl, do_mixer, dbg, tuple(branches))
    if key not in _CACHE:
        _CACHE[key] = Builder(nl, do_mixer, dbg, tuple(branches)).build()
    nc = _CACHE[key]
    x = np.asarray(inputs["x"], np.float32)
    meta = np.asarray(inputs["meta"], np.float32)
    B = x.shape[0]
    pv = pack_params(inputs, nl)
    cst = make_consts()
    shared = {"pvec": pv, "cst": cst}
    for k in ("ffn1_w13", "ffn1_w2", "ffn2_w13", "ffn2_w2", "w_in", "w_out", "rwkv_wB", "rwkv_aB", "rwkv_gB"):
        shared[k] = np.ascontiguousarray(np.asarray(inputs[k], np.float32)[:nl])
    in_maps = []
    for b in range(B):
        xp = np.zeros((LP, D), np.float32)
        xp[:NMETA] = meta
        xp[NMETA:NMETA + SEQ] = x[b]
        m = dict(shared)
        m["xpad"] = xp
        in_maps.append(m)
    res = run_bass_kernel_spmd(nc, in_maps, core_ids=list(range(B)))
    out = np.stack([r["out"] for r in res.results], axis=0)
    if dbg:
        return out, [r["dbg"] for r in res.results]
    return out


def kernel(**inputs):
    return run(inputs)
```
